# Optimizing a Trainium2 kernel written in Bass

```python
import jax, jax.numpy as jnp
from jax import lax
import numpy as np

D_MODEL = 1024
BATCH = 4
SEQ = 8192
DEPTH = 2
DEC_BATCH = 128
DEC_SEQ = 4
PAST_LEN = 16384
PAGE_SIZE = 128

N_HEADS = 16
N_KV_HEADS = 4
HEAD_DIM = 64
GROUP = N_HEADS // N_KV_HEADS
WINDOW = 128
BLOCK = WINDOW
D_RNN = D_MODEL
N_LRU_BLOCKS = 4
LRU_BLOCK = D_RNN // N_LRU_BLOCKS
CONV_WIDTH = 4
LRU_C = 8.0
D_FF = -(-8 * D_MODEL // (3 * 256)) * 256
N_ATTN_LAYERS = (DEPTH + 1) // 2
N_REC_LAYERS = DEPTH // 2
EPS = 1e-6
NEG_INF = -1e30

kernel_name = "hybrid_swa_sink_rglru_decoder_step"


def rms_norm(x, g):
    xf = x.astype(jnp.float32)
    y = xf * lax.rsqrt(jnp.mean(xf * xf, axis=-1, keepdims=True) + EPS)
    return (y * g.astype(jnp.float32)).astype(x.dtype)


def alibi_slopes():
    return 2.0 ** (-8.0 * jnp.arange(1, N_HEADS + 1, dtype=jnp.float32) / N_HEADS)


def qkv_proj(h, w_qkv):
    B, T, _ = h.shape
    qkv = h @ w_qkv
    q = qkv[..., :N_HEADS * HEAD_DIM].reshape(B, T, N_KV_HEADS, GROUP, HEAD_DIM)
    k = qkv[..., N_HEADS * HEAD_DIM:(N_HEADS + N_KV_HEADS) * HEAD_DIM].reshape(B, T, N_KV_HEADS, HEAD_DIM)
    v = qkv[..., (N_HEADS + N_KV_HEADS) * HEAD_DIM:].reshape(B, T, N_KV_HEADS, HEAD_DIM)
    return q, k, v


def sink_attend(q, k, v, dist, valid, sinks):
    s = jnp.einsum('...qkgd,...skd->...kgqs', q, k, preferred_element_type=jnp.float32) * (HEAD_DIM ** -0.5)
    slopes = alibi_slopes().reshape(N_KV_HEADS, GROUP, 1, 1)
    s = jnp.where(valid, s - slopes * dist.astype(jnp.float32), NEG_INF)
    sink = jnp.broadcast_to(sinks.astype(jnp.float32).reshape(N_KV_HEADS, GROUP, 1, 1), s.shape[:-1] + (1,))
    p = jax.nn.softmax(jnp.concatenate([s, sink], axis=-1), axis=-1)[..., :-1]
    return jnp.einsum('...kgqs,...skd->...qkgd', p.astype(v.dtype), v)


def swa_prompt(h, w_qkv, w_o, sinks):
    B, S, _ = h.shape
    nb = S // BLOCK
    q, k, v = qkv_proj(h, w_qkv)
    qb = q.reshape(B, nb, BLOCK, N_KV_HEADS, GROUP, HEAD_DIM)
    kb = k.reshape(B, nb, BLOCK, N_KV_HEADS, HEAD_DIM)
    vb = v.reshape(B, nb, BLOCK, N_KV_HEADS, HEAD_DIM)
    zk = jnp.zeros_like(kb[:, :1])
    kk = jnp.concatenate([jnp.concatenate([zk, kb[:, :-1]], axis=1), kb], axis=2)
    vv = jnp.concatenate([jnp.concatenate([zk, vb[:, :-1]], axis=1), vb], axis=2)
    qi = jnp.arange(BLOCK)[:, None]
    si = jnp.arange(2 * BLOCK)[None, :]
    dist = qi - si + BLOCK
    band = (dist >= 0) & (dist <= WINDOW)
    key_pos = jnp.arange(nb)[:, None, None] * BLOCK - BLOCK + si
    valid = (band[None] & (key_pos >= 0))[:, None, None]
    o = sink_attend(qb, kk, vv, dist, valid, sinks)
    y = o.reshape(B, S, N_HEADS * HEAD_DIM) @ w_o
    return y, k[:, S - WINDOW:], v[:, S - WINDOW:]


def swa_sample(h, cache_k, cache_v, w_qkv, w_o, sinks):
    B, T, _ = h.shape
    q, k, v = qkv_proj(h, w_qkv)
    kk = jnp.concatenate([cache_k.astype(k.dtype), k], axis=1)
    vv = jnp.concatenate([cache_v.astype(v.dtype), v], axis=1)
    qi = jnp.arange(T)[:, None]
    si = jnp.arange(WINDOW + T)[None, :]
    dist = qi - si + WINDOW
    valid = (dist >= 0) & (dist <= WINDOW)
    o = sink_attend(q, kk, vv, dist, valid, sinks)
    y = o.reshape(B, T, N_HEADS * HEAD_DIM) @ w_o
    return y, kk[:, T:], vv[:, T:]


def lru_combine(left, right):
    a1, b1 = left
    a2, b2 = right
    return a1 * a2, a2 * b1 + b2


def recurrent_block(h, conv_state, h0, w_in, conv_w, conv_b, w_rg, b_rg, w_ig, b_ig, lam, w_out):
    B, T, _ = h.shape
    gate, xb = jnp.split(h @ w_in, 2, axis=-1)
    xp = jnp.concatenate([conv_state.astype(xb.dtype), xb], axis=1)
    xc = sum((xp[:, j:j + T] * conv_w[j] for j in range(CONV_WIDTH)), conv_b)
    xr = xc.reshape(B, T, N_LRU_BLOCKS, LRU_BLOCK)
    r = jax.nn.sigmoid(jnp.einsum('btnc,ncd->btnd', xr, w_rg) + b_rg.reshape(N_LRU_BLOCKS, LRU_BLOCK)).reshape(B, T, D_RNN)
    i = jax.nn.sigmoid(jnp.einsum('btnc,ncd->btnd', xr, w_ig) + b_ig.reshape(N_LRU_BLOCKS, LRU_BLOCK)).reshape(B, T, D_RNN)
    log_a = -LRU_C * r.astype(jnp.float32) * jax.nn.softplus(-lam.astype(jnp.float32))
    a = jnp.exp(log_a)
    b = jnp.sqrt(-jnp.expm1(2.0 * log_a)) * (i * xc).astype(jnp.float32)
    b = b.at[:, 0].add(a[:, 0] * h0.astype(jnp.float32))
    _, hs = lax.associative_scan(lru_combine, (a, b), axis=1)
    y = (hs.astype(h.dtype) * jax.nn.gelu(gate)) @ w_out
    return y, xp[:, T:], hs[:, -1]


def swiglu(h, w_in, w_out):
    g, u = jnp.split(h @ w_in, 2, axis=-1)
    return (jax.nn.silu(g) * u) @ w_out


def setup_inputs(seed: int = 0) -> dict:
    key = jax.random.key(seed)
    ks = jax.random.split(key, 24)
    f32 = jnp.float32
    nrm = lambda k, shape, s: jax.random.normal(k, shape, f32) * s
    qkv_w = (N_HEADS + 2 * N_KV_HEADS) * HEAD_DIM
    u = jax.random.uniform(ks[20], (N_REC_LAYERS, D_RNN), f32, minval=0.9, maxval=0.999)
    a_base = u ** (1.0 / LRU_C)
    return {
        "x_prompt": nrm(ks[0], (BATCH, SEQ, D_MODEL), 1.0),
        "x_sample": nrm(ks[1], (DEC_BATCH, DEC_SEQ, D_MODEL), 1.0),
        "cache_k": nrm(ks[2], (N_ATTN_LAYERS, DEC_BATCH, WINDOW, N_KV_HEADS, HEAD_DIM), 1.0),
        "cache_v": nrm(ks[3], (N_ATTN_LAYERS, DEC_BATCH, WINDOW, N_KV_HEADS, HEAD_DIM), 1.0),
        "state_conv": nrm(ks[4], (N_REC_LAYERS, DEC_BATCH, CONV_WIDTH - 1, D_RNN), 1.0),
        "state_h": nrm(ks[5], (N_REC_LAYERS, DEC_BATCH, D_RNN), 0.5),
        "attn_norm": 1.0 + nrm(ks[6], (N_ATTN_LAYERS, D_MODEL), 0.02),
        "w_qkv": nrm(ks[7], (N_ATTN_LAYERS, D_MODEL, qkv_w), D_MODEL ** -0.5),
        "w_attn_out": nrm(ks[8], (N_ATTN_LAYERS, N_HEADS * HEAD_DIM, D_MODEL), (N_HEADS * HEAD_DIM) ** -0.5),
        "attn_sinks": nrm(ks[9], (N_ATTN_LAYERS, N_HEADS), 1.0),
        "rec_norm": 1.0 + nrm(ks[10], (N_REC_LAYERS, D_MODEL), 0.02),
        "w_rec_in": nrm(ks[11], (N_REC_LAYERS, D_MODEL, 2 * D_RNN), D_MODEL ** -0.5),
        "conv_w": nrm(ks[12], (N_REC_LAYERS, CONV_WIDTH, D_RNN), CONV_WIDTH ** -0.5),
        "conv_b": nrm(ks[13], (N_REC_LAYERS, D_RNN), 0.1),
        "w_rgate": nrm(ks[14], (N_REC_LAYERS, N_LRU_BLOCKS, LRU_BLOCK, LRU_BLOCK), LRU_BLOCK ** -0.5),
        "b_rgate": nrm(ks[15], (N_REC_LAYERS, D_RNN), 0.1),
        "w_igate": nrm(ks[16], (N_REC_LAYERS, N_LRU_BLOCKS, LRU_BLOCK, LRU_BLOCK), LRU_BLOCK ** -0.5),
        "b_igate": nrm(ks[17], (N_REC_LAYERS, D_RNN), 0.1),
        "lru_lambda": jnp.log(a_base) - jnp.log1p(-a_base),
        "w_rec_out": nrm(ks[18], (N_REC_LAYERS, D_RNN, D_MODEL), D_RNN ** -0.5),
        "ffn_norm": 1.0 + nrm(ks[19], (DEPTH, D_MODEL), 0.02),
        "w_ffn_in": nrm(ks[21], (DEPTH, D_MODEL, 2 * D_FF), D_MODEL ** -0.5),
        "w_ffn_out": nrm(ks[22], (DEPTH, D_FF, D_MODEL), D_FF ** -0.5),
        "final_norm": 1.0 + nrm(ks[23], (D_MODEL,), 0.02),
    }


def reference(x_prompt, x_sample, cache_k, cache_v, state_conv, state_h, attn_norm, w_qkv, w_attn_out, attn_sinks, rec_norm, w_rec_in, conv_w, conv_b, w_rgate, b_rgate, w_igate, b_igate, lru_lambda, w_rec_out, ffn_norm, w_ffn_in, w_ffn_out, final_norm):
    xp, xs = x_prompt, x_sample
    nk_p, nv_p, nk_s, nv_s = [], [], [], []
    nc_p, nh_p, nc_s, nh_s = [], [], [], []
    for layer in range(DEPTH):
        j = layer // 2
        if layer % 2 == 0:
            yp, kp, vp = swa_prompt(rms_norm(xp, attn_norm[j]), w_qkv[j], w_attn_out[j], attn_sinks[j])
            ys, ks_, vs_ = swa_sample(rms_norm(xs, attn_norm[j]), cache_k[j], cache_v[j], w_qkv[j], w_attn_out[j], attn_sinks[j])
            nk_p.append(kp); nv_p.append(vp); nk_s.append(ks_); nv_s.append(vs_)
        else:
            rec_w = (w_rec_in[j], conv_w[j], conv_b[j], w_rgate[j], b_rgate[j], w_igate[j], b_igate[j], lru_lambda[j], w_rec_out[j])
            zc = jnp.zeros((xp.shape[0], CONV_WIDTH - 1, D_RNN), xp.dtype)
            zh = jnp.zeros((xp.shape[0], D_RNN), jnp.float32)
            yp, cp, hp = recurrent_block(rms_norm(xp, rec_norm[j]), zc, zh, *rec_w)
            ys, cs, hs = recurrent_block(rms_norm(xs, rec_norm[j]), state_conv[j], state_h[j], *rec_w)
            nc_p.append(cp); nh_p.append(hp); nc_s.append(cs); nh_s.append(hs)
        xp = xp + yp
        xs = xs + ys
        xp = xp + swiglu(rms_norm(xp, ffn_norm[layer]), w_ffn_in[layer], w_ffn_out[layer])
        xs = xs + swiglu(rms_norm(xs, ffn_norm[layer]), w_ffn_in[layer], w_ffn_out[layer])
    y_prompt = rms_norm(xp, final_norm)
    y_sample = rms_norm(xs, final_norm)
    return (y_prompt, y_sample, jnp.stack(nk_p), jnp.stack(nv_p), jnp.stack(nk_s), jnp.stack(nv_s), jnp.stack(nc_p), jnp.stack(nh_p), jnp.stack(nc_s), jnp.stack(nh_s))
```

```python
import contextlib
import os
import numpy as np
import concourse.bass as bass
import concourse.mybir as mybir
from concourse.bass_utils import run_bass_kernel_spmd

F32 = mybir.dt.float32
BF16 = mybir.dt.bfloat16
AF = mybir.ActivationFunctionType
ALU = mybir.AluOpType

ENGS = ("pe", "act", "dve", "pool", "sp")


class Res:
    __slots__ = ("name", "w", "r", "lsem", "ssem")

    def __init__(self, name):
        self.name = name
        self.w = None
        self.r = []
        self.lsem = None
        self.ssem = None


class Ev:
    __slots__ = ("kind", "key", "val", "clock", "op")

    def __init__(self, kind, key, val, clock, op):
        self.kind = kind
        self.key = key
        self.val = val
        self.clock = clock
        self.op = op


class OpRec:
    __slots__ = ("eng", "fn", "waits", "marked", "idx", "dma", "semslot", "inc")

    def __init__(self, eng, fn, idx):
        self.eng = eng
        self.fn = fn
        self.waits = []
        self.marked = False
        self.idx = idx
        self.dma = False
        self.semslot = None
        self.inc = 16


class Prog:
    def __init__(self, nc):
        self.nc = nc
        self.ops = {e: [] for e in ENGS}
        self.clock = {e: {} for e in ENGS}
        self.dsem_count = []
        self.dsem_last = {}

    def _need(self, eng, ev):
        if ev.kind == 'c' and ev.key[1] == eng and eng == 'pe':
            return False
        return self.clock[eng].get(ev.key, -1) < ev.val

    def _merge(self, eng, ev):
        c = self.clock[eng]
        for k, v in ev.clock.items():
            if c.get(k, -1) < v:
                c[k] = v
        if c.get(ev.key, -1) < ev.val:
            c[ev.key] = ev.val

    def op(self, eng, fn, reads=(), writes=(), dma=False, inc=16):
        rec = OpRec(eng, fn, len(self.ops[eng]))
        rec.dma = dma
        rec.inc = inc
        deps = []
        for r in reads:
            if r.w is not None:
                deps.append(r.w)
        for w in writes:
            if w.w is not None:
                deps.append(w.w)
            deps.extend(w.r)
        deps.sort(key=lambda ev: -ev.val)
        for ev in deps:
            if self._need(eng, ev):
                rec.waits.append(ev)
                if ev.kind == 'c':
                    ev.op.marked = True
                self._merge(eng, ev)
        if dma:
            if writes:
                tgt = writes[0]
                if tgt.lsem is None:
                    self.dsem_count.append(0)
                    tgt.lsem = len(self.dsem_count) - 1
                slot = tgt.lsem
            else:
                tgt = reads[0]
                if tgt.ssem is None:
                    self.dsem_count.append(0)
                    tgt.ssem = len(self.dsem_count) - 1
                slot = tgt.ssem
            last = self.dsem_last.get(slot)
            if last is not None and self._need(eng, last):
                rec.waits.append(last)
                self._merge(eng, last)
            self.dsem_count[slot] += inc
            rec.semslot = slot
            ev = Ev('d', ('d', slot), self.dsem_count[slot], dict(self.clock[eng]), rec)
            self.dsem_last[slot] = ev
        else:
            ev = Ev('c', ('c', eng), rec.idx, dict(self.clock[eng]), rec)
        self.ops[eng].append(rec)
        for r in reads:
            r.r.append(ev)
        for w in writes:
            w.w = ev
            w.r = []
        return ev

    def final_wait(self, eng, evs):
        rec = OpRec(eng, None, len(self.ops[eng]))
        for ev in sorted(evs, key=lambda ev: -ev.val):
            if self._need(eng, ev):
                rec.waits.append(ev)
                if ev.kind == 'c':
                    ev.op.marked = True
                self._merge(eng, ev)
        self.ops[eng].append(rec)

    def emit(self):
        nc = self.nc
        with contextlib.ExitStack() as st:
            esem = {e: st.enter_context(nc.semaphore("s_" + e)) for e in ENGS}
            dsem = [st.enter_context(nc.semaphore("d%d" % i)) for i in range(len(self.dsem_count))]
            rank = {}
            for e in ENGS:
                n = 0
                for rec in self.ops[e]:
                    if rec.marked:
                        n += 1
                    rank[(e, rec.idx)] = n
            block = st.enter_context(nc.Block())

            def run(e, eng):
                for rec in self.ops[e]:
                    for ev in rec.waits:
                        if ev.kind == 'c':
                            eng.wait_ge(esem[ev.key[1]], rank[(ev.key[1], ev.val)])
                        else:
                            eng.wait_ge(dsem[ev.key[1]], ev.val)
                    if rec.fn is None:
                        continue
                    ins = rec.fn(eng)
                    if rec.dma:
                        ins.then_inc(dsem[rec.semslot], rec.inc)
                    elif rec.marked:
                        ins.then_inc(esem[e], 1)

            @block.tensor
            def _(eng):
                run("pe", eng)

            @block.scalar
            def _(eng):
                run("act", eng)

            @block.vector
            def _(eng):
                run("dve", eng)

            @block.gpsimd
            def _(eng):
                run("pool", eng)

            @block.sync
            def _(eng):
                run("sp", eng)


D = 1024
KC = 8
DFF = 2816
FC = 22
T = 512
NQB = 4
NH = 16
NKV = 4
HD = 64
SEQ = 8192
BATCH = 4
DEC_B = 128
DEC_T = 4
EPS = 1e-6
NEG = -30000.0
SLOT = 4096
NSLOT = 5
NFROT = 8
SB = 16
NS = 64

V_AN, V_FN0, V_RN, V_FN1, V_FIN = 0, 8, 16, 24, 32
V_CW, V_CB, V_BRG, V_BIG, V_LAM, V_SINK = 40, 72, 80, 88, 96, 104
NV = 120

W_ORDER = ["qkv", "wo", "fi0", "fo0", "ri", "gt", "ro", "fi1", "fo1"]
W_PIECES = {"qkv": 3, "wo": 2, "fi0": 11, "fo0": 8, "ri": 4, "gt": 1, "ro": 2, "fi1": 11, "fo1": 8}
W_PSZ = {"qkv": 4096, "wo": 4096, "fi0": 4096, "fo0": 2816, "ri": 4096, "gt": 4096, "ro": 4096, "fi1": 4096, "fo1": 2816}


class Builder:
    def __init__(self, n_tiles, sample=True):
        self.n_tiles = n_tiles
        self.sample = sample
        self.nc = bass.Bass("TRN2", target_bir_lowering=False)
        self.st = contextlib.ExitStack()
        self.P = Prog(self.nc)
        self._psum_i = 0
        self._f_i = 0
        self._sq_i = 0
        self._pt_i = 0
        self.out_evs = []
        self.n = T

    def dram(self, name, shape, dt=F32, kind="ExternalInput"):
        return self.nc.dram_tensor(name, list(shape), dt, kind=kind).ap()

    def sb(self, name, shape, dt=F32):
        return self.st.enter_context(self.nc.sbuf_tensor(name, list(shape), dt))

    def psum(self):
        i = self._psum_i % 8
        self._psum_i += 1
        return self.ps[i], self.ps_r[i]

    def ftmp(self):
        i = self._f_i % NFROT
        self._f_i += 1
        return self.F[i], self.F_r[i]

    def op(self, eng, fn, reads=(), writes=(), **kw):
        return self.P.op(eng, fn, reads, writes, **kw)

    def act(self, out, in_, func, reads, writes, **kw):
        return self.P.op("act", lambda e: e.activation(out=out, in_=in_, func=func, **kw), reads, writes)

    def mm(self, out, lhsT, rhs, start, stop, reads, writes):
        return self.P.op("pe", lambda e: e.matmul(out, lhsT=lhsT, rhs=rhs, start=start, stop=stop), reads, writes)

    def tt(self, eng, out, in0, in1, op, reads, writes):
        return self.P.op(eng, lambda e: e.tensor_tensor(out=out, in0=in0, in1=in1, op=op), reads, writes)

    def stt(self, out, in0, scalar, in1, op0, op1, reads, writes):
        return self.P.op("dve", lambda e: e.scalar_tensor_tensor(out=out, in0=in0, scalar=scalar, in1=in1, op0=op0, op1=op1), reads, writes)

    def cp(self, eng, out, in_, reads, writes):
        if eng == "act":
            return self.act(out, in_, AF.Copy, reads, writes)
        return self.P.op(eng, lambda e: e.tensor_copy(out=out, in_=in_), reads, writes)

    def dma(self, eng, out, in_, reads, writes, **kw):
        return self.P.op(eng, lambda e: e.dma_start(out=out, in_=in_, **kw), reads, writes, dma=True)

    def declare(self):
        nt = self.n_tiles
        self.xp = self.dram("xp", [nt * T, D])
        self.vecs_d = self.dram("vecs", [128, NV])
        self.btab_d = self.dram("btab", [128, 8 * 512])
        self.hb_d = self.dram("hb", [128, 1])
        self.xpre = self.dram("xpre", [nt * T, D])
        self.flag_d = self.dram("flag", [128, 1])
        self.sinkT_d = self.dram("sinkT", [4, 4])
        self.w_in = {k: self.dram("w_" + k, [128, W_PIECES[k] * W_PSZ[k]]) for k in W_ORDER}
        self.w_sc = {k: self.dram("s_" + k, [128, W_PIECES[k] * W_PSZ[k]], BF16, kind="Internal") for k in W_ORDER}
        self.yp = self.dram("yp", [nt * T, D], kind="ExternalOutput")
        self.nk = self.dram("nk", [128, 256], kind="ExternalOutput")
        self.nv = self.dram("nv", [128, 256], kind="ExternalOutput")
        self.ncv = self.dram("ncv", [3, D], kind="ExternalOutput")
        self.nh = self.dram("nh", [D], kind="ExternalOutput")
        if self.sample:
            self.xs_d = self.dram("xs", [NS, D])
            self.ckT_d = self.dram("ckT", [128, SB * 2 * 128])
            self.ck_d = self.dram("ck", [SB, 128, 256])
            self.cv_d = self.dram("cv", [SB, 128, 256])
            self.sconv_d = self.dram("sconv", [128, KC * SB * 3])
            self.sh0_d = self.dram("sh0", [128, KC * SB])
            self.bts_d = self.dram("bts", [128, 128])
            self.ys = self.dram("ys", [NS, D], kind="ExternalOutput")
            self.nks = self.dram("nks", [SB, 128, 256], kind="ExternalOutput")
            self.nvs = self.dram("nvs", [SB, 128, 256], kind="ExternalOutput")
            self.ncs = self.dram("ncs", [SB, 3, D], kind="ExternalOutput")
            self.nhs = self.dram("nhs", [SB, D], kind="ExternalOutput")
            self.kscr = self.dram("kscr", [NS, 256], kind="Internal")
            self.vscr = self.dram("vscr", [NS, 256], kind="Internal")
            self.xbscr = self.dram("xbscr", [NS, D], kind="Internal")

        sb = self.sb
        self.vecs = sb("vecs_sb", [128, NV]); self.vecs_r = Res("vecs")
        self.der = sb("der_sb", [128, 48]); self.der_r = Res("der")
        self.btab = sb("btab_sb", [128, 8, 512]); self.btab_r = Res("btab")
        self.hb = sb("hb_sb", [128, 1]); self.hb_r = Res("hb")
        self.flag = sb("flag_sb", [128, 1]); self.flag_r = Res("flag")
        self.snk = sb("snk", [4, NKV, 128], BF16); self.ind = sb("ind", [4, 512], BF16); self.esT = sb("esT", [4, 4]); self.snk_r = Res("snk")
        self.ident = sb("ident", [128, 128]); self.ident_r = Res("ident")
        self.ones = sb("ones_bf", [128, 128], BF16); self.ones_r = Res("ones")
        self.cst = sb("cst", [128, 4]); self.cst_r = Res("cst")
        self.xin = sb("xin", [128, 2, D]); self.xin_r = [Res("xin%d" % i) for i in range(2)]
        self.yout = sb("yout", [128, 2, D]); self.yout_r = [Res("yout%d" % i) for i in range(2)]
        self.xres = sb("xres", [128, KC, T]); self.xres_r = [Res("xres%d" % i) for i in range(KC)]
        self.h = sb("h", [128, KC, T], BF16); self.h_r = [Res("h%d" % i) for i in range(KC)]
        self.sq = sb("sq", [128, 2, T], BF16); self.sq_r = [Res("sq%d" % i) for i in range(2)]
        self.U = sb("U", [128, 24, T], BF16); self.U_r = [Res("U%d" % i) for i in range(24)]
        self.kT = sb("kT", [128, 2, 128 + T], BF16)
        self.kT_r = [Res("kT%d" % i) for i in range(2)]; self.kTh_r = [Res("kTh%d" % i) for i in range(2)]
        self.va = sb("vaug", [128, 5, NKV, 128], BF16); self.va_r = [Res("va%d" % i) for i in range(5)]
        self.xbt = sb("xbt", [128, KC, 3 + T]); self.xbt_r = [Res("xbt%d" % i) for i in range(KC)]
        self.xbh_r = [Res("xbh%d" % i) for i in range(KC)]
        self.hcar = sb("hcar", [128, KC]); self.hcar_r = [Res("hcar%d" % i) for i in range(KC)]
        self.F = [sb("F%d" % i, [128, T]) for i in range(NFROT)]; self.F_r = [Res("F%d" % i) for i in range(NFROT)]
        self.trT = sb("trT", [128, KC, T]); self.tr_r = [Res("tr%d" % i) for i in range(KC)]
        self.xcT = sb("xcT", [128, KC, T]); self.xc_r = [Res("xc%d" % i) for i in range(KC)]
        self.tr = [self.trT[:, i, :] for i in range(KC)]
        self.xc = [self.xcT[:, i, :] for i in range(KC)]
        self.kv32 = sb("kv32", [128, 512]); self.kv32_r = Res("kv32"); self.kv32b_r = Res("kv32b")
        self.wslot = [sb("wslot%d" % i, [128, SLOT], BF16) for i in range(NSLOT)]
        self.wslot_r = [Res("wslot%d" % i) for i in range(NSLOT)]
        if self.sample:
            self.q32 = sb("q32_sb", [128, KC, NS]); self.q32_r = Res("q32")
            self.kn32 = sb("kn32_sb", [128, 2, NS]); self.kn32_r = Res("kn32")
            self.xps = sb("xps_sb", [128, KC, SB, 7]); self.xps_r = [Res("xps%d" % i) for i in range(KC)]; self.xpsh_r = Res("xpsh")
            self.h0s = sb("h0s_sb", [128, KC, SB]); self.h0s_r = Res("h0s")
            self.hl = sb("hl_sb", [128, KC, SB]); self.hl_r = [Res("hl%d" % i) for i in range(KC)]
            self.bts = sb("bts_sb", [128, 128]); self.bts_r = Res("bts")
            self.ones32 = sb("ones32", [128, 64]); self.ones32_r = Res("ones32")
        self.ps = [self.st.enter_context(self.nc.psum_tensor("ps%d" % i, [128, 512], F32)) for i in range(8)]
        self.ps_r = [Res("ps%d" % i) for i in range(8)]

    def setup_weights(self):
        self.wchunk_r = {}
        stA = self.xres[:].rearrange("p c t -> p (c t)")
        stB = self.xcT[:].rearrange("p c t -> p (c t)")
        stage = [(stA, list(self.xres_r)), (stB, list(self.xc_r))]
        i = 0
        for k in W_ORDER:
            psz = W_PSZ[k]
            for pi in range(W_PIECES[k]):
                st_ap, st_res = stage[i % 2]
                s = i % NSLOT
                if pi == 0:
                    rk = Res("wc_%s" % k)
                r = rk
                self.dma("sp", st_ap[:, 0:psz], self.w_in[k][:, pi * psz:(pi + 1) * psz], [], st_res)
                a, b = (psz // 3) // 128 * 128, (2 * psz // 3) // 128 * 128
                for eng, c0, c1 in (("act", 0, a), ("dve", a, b), ("pool", b, psz)):
                    self.cp(eng, self.wslot[s][:, c0:c1], st_ap[:, c0:c1], st_res, [self.wslot_r[s]])
                self.dma("act", self.w_sc[k][:, pi * psz:(pi + 1) * psz], self.wslot[s][:, 0:psz], [self.wslot_r[s]], [r])
                self.wchunk_r[(k, pi)] = r
                i += 1
        self.wseq = []
        for ti in range(self.n_tiles):
            for k in ["qkv", "wo", "fi0", "fo0"]:
                for i in range(W_PIECES[k]):
                    self.wseq.append((k, i))
            self.wseq += [("ri", 2), ("ri", 3), ("gt", 0)]
        for ti in range(self.n_tiles + (1 if self.sample else 0)):
            for k in W_ORDER:
                for i in range(W_PIECES[k]):
                    self.wseq.append((k, i))
        self.w_issued = 0
        self.w_used = 0

    def _issue_w(self):
        idx = self.w_issued
        k, i = self.wseq[idx]
        s = idx % NSLOT
        psz = W_PSZ[k]
        self.dma("sp", self.wslot[s][:, 0:psz], self.w_sc[k][:, i * psz:(i + 1) * psz],
                 [self.wchunk_r[(k, i)]], [self.wslot_r[s]])
        self.w_issued += 1

    def wnext(self, k, i):
        idx = self.w_used
        assert self.wseq[idx] == (k, i), (self.wseq[idx], k, i)
        while self.w_issued < min(len(self.wseq), idx + NSLOT):
            self._issue_w()
        self.w_used += 1
        s = idx % NSLOT
        return self.wslot[s], self.wslot_r[s]

    def setup_consts(self):
        self.dma("sp", self.vecs[:], self.vecs_d, [], [self.vecs_r])
        self.dma("sp", self.hb[:], self.hb_d, [], [self.hb_r])
        self.dma("sp", self.flag[:], self.flag_d, [], [self.flag_r])
        snk, ind, esT = self.snk, self.ind, self.esT
        self.dma("sp", esT[:], self.sinkT_d, [], [self.snk_r])
        self.act(esT[:], esT[:], AF.Exp, [self.snk_r], [self.snk_r])
        self.op("pool", lambda e: e.memset(snk[:], 0.0), [self.snk_r], [self.snk_r])
        for j in range(NKV):
            d0 = (1 - j % 2) * 64
            self.cp("dve", snk[:, j, d0:d0 + 64], esT[:, j:j + 1].to_broadcast([4, 64]), [self.snk_r], [self.snk_r])
        self.op("pool", lambda e: e.memset(ind[:], 1.0), [self.snk_r], [self.snk_r])
        self.op("pool", lambda e: e.affine_select(out=ind[:], in_=ind[:], pattern=[[1, 512]], compare_op=ALU.is_ge, fill=0.0, base=0, channel_multiplier=-128),
                [self.snk_r], [self.snk_r])
        self.op("pool", lambda e: e.affine_select(out=ind[:], in_=ind[:], pattern=[[-1, 512]], compare_op=ALU.is_ge, fill=0.0, base=127, channel_multiplier=128),
                [self.snk_r], [self.snk_r])
        self.dma("sp", self.btab[:].rearrange("p a b -> p (a b)"), self.btab_d, [], [self.btab_r])
        op = self.op
        ident, ones, cst = self.ident, self.ones, self.cst
        op("pool", lambda e: e.memset(ident[:], 0.0), [], [self.ident_r])
        op("pool", lambda e: e.affine_select(out=ident[:], in_=ident[:], pattern=[[-1, 128]], compare_op=ALU.not_equal,
                                             fill=1.0, base=0, channel_multiplier=1), [self.ident_r], [self.ident_r])
        op("pool", lambda e: e.memset(ones[:], 1.0), [], [self.ones_r])
        op("pool", lambda e: e.memset(cst[:, 0:1], EPS), [], [self.cst_r])
        op("pool", lambda e: e.memset(cst[:, 1:2], 1.0), [self.cst_r], [self.cst_r])
        op("pool", lambda e: e.memset(cst[:, 2:3], 0.0), [self.cst_r], [self.cst_r])
        op("pool", lambda e: e.memset(cst[:, 3:4], NEG), [self.cst_r], [self.cst_r])
        kT, va, xbt, hcar = self.kT, self.va, self.xbt, self.hcar
        for cc in range(2):
            op("pool", (lambda cc: lambda e: e.memset(kT[:, cc, 0:128], 0.0))(cc), [], [self.kTh_r[cc]])
        for b in range(5):
            op("pool", (lambda b: lambda e: e.memset(va[:, b, :, :], 1.0 if b else 0.0))(b), [], [self.va_r[b]])
        op("pool", lambda e: e.memset(va[:, 0, 0::2, 64:128], 1.0), [self.va_r[0]], [self.va_r[0]])
        op("pool", lambda e: e.memset(va[:, 0, 1::2, 0:64], 1.0), [self.va_r[0]], [self.va_r[0]])
        for c in range(KC):
            op("pool", (lambda c: lambda e: e.memset(xbt[:, c, 0:3], 0.0))(c), [], [self.xbh_r[c]])
            op("pool", (lambda c: lambda e: e.memset(hcar[:, c:c + 1], 0.0))(c), [], [self.hcar_r[c]])
        der, vecs = self.der, self.vecs
        self.act(der[:, 0:8], vecs[:, V_LAM:V_LAM + 8], AF.Exp, [self.vecs_r], [self.der_r], scale=-1.0)
        self.act(der[:, 0:8], der[:, 0:8], AF.Ln, [self.der_r, self.cst_r], [self.der_r], bias=cst[:, 1:2])
        op("dve", lambda e: e.tensor_scalar(out=der[:, 0:8], in0=der[:, 0:8], scalar1=-4.0, scalar2=None, op0=ALU.mult),
           [self.der_r], [self.der_r])
        op("dve", lambda e: e.tensor_scalar(out=der[:, 8:24], in0=vecs[:, V_BRG:V_BRG + 16], scalar1=0.5, scalar2=None, op0=ALU.mult),
           [self.vecs_r, self.der_r], [self.der_r])
        self.act(der[:, 24:40], vecs[:, V_SINK:V_SINK + 16], AF.Exp, [self.vecs_r, self.der_r], [self.der_r])
        if self.sample:
            ones32 = self.ones32
            op("pool", lambda e: e.memset(ones32[:], 1.0), [], [self.ones32_r])
            self.dma("sp", self.bts[:], self.bts_d, [], [self.bts_r])
            self.dma("sp", self.h0s[:].rearrange("p c s -> p (c s)"), self.sh0_d, [], [self.h0s_r])
            self.cache_r = [Res("nks_c"), Res("nvs_c")]
            self.out_evs.append(self.dma("act", self.nks[:, 0:124, :], self.ck_d[:, 4:128, :], [], [self.cache_r[0]]))
            self.out_evs.append(self.dma("act", self.nvs[:, 0:124, :], self.cv_d[:, 4:128, :], [], [self.cache_r[1]]))

    def load_x(self, ti, src=None):
        xin = self.xin
        src = self.xp if src is None else src
        for pr in range(NQB // 2):
            for s in range(2):
                b = 2 * pr + s
                self.dma("sp", xin[:, s, :], src[ti * T + b * 128: ti * T + (b + 1) * 128, :], [], [self.xin_r[s]])
            self._x_transposes(2 * pr, 2)

    def _x_transposes(self, b0, nb):
        xin, xres, ident = self.xin, self.xres, self.ident
        for c in range(KC):
            bank, br = self.psum()
            for j in range(nb):
                self.op("pe", (lambda bank, j, c: lambda e: e.transpose(bank[:, j * 128:(j + 1) * 128], xin[:, j, c * 128:(c + 1) * 128], ident[:]))(bank, j, c),
                        [self.xin_r[j], self.ident_r], [br])
            eng = "act" if c % 2 == 0 else "dve"
            self.cp(eng, xres[:, c, b0 * 128:(b0 + nb) * 128], bank[:, 0:nb * 128], [br], [self.xres_r[c]])

    def load_xs(self):
        xin, xres, ident = self.xin, self.xres, self.ident
        self.dma("sp", xin[0:NS, 0, :], self.xs_d, [], [self.xin_r[0]])
        for c in range(KC):
            bank, br = self.psum()
            self.op("pe", (lambda bank, c: lambda e: e.transpose(bank[:, 0:NS], xin[0:NS, 0, c * 128:(c + 1) * 128], ident[0:NS, 0:NS]))(bank, c),
                    [self.xin_r[0], self.ident_r], [br])
            self.cp("act" if c % 2 == 0 else "dve", xres[:, c, 0:NS], bank[:, 0:NS], [br], [self.xres_r[c]])

    def rmsnorm(self, gcol, inplace=False):
        n = self.n
        xres, sq, ones, vecs, cst = self.xres, self.sq, self.ones, self.vecs, self.cst
        bank, br = self.psum()
        for c in range(KC):
            s = self._sq_i % 2
            self._sq_i += 1
            self.tt("pool", sq[:, s, 0:n], xres[:, c, 0:n], xres[:, c, 0:n], ALU.mult, [self.xres_r[c]], [self.sq_r[s]])
            self.mm(bank[:, 0:n], ones[:], sq[:, s, 0:n], c == 0, c == KC - 1, [self.ones_r, self.sq_r[s]], [br])
        f1, f1r = self.ftmp()
        self.act(f1[:, 0:n], bank[:, 0:n], AF.Ln, [br, self.cst_r], [f1r], scale=1.0 / D, bias=cst[:, 0:1])
        self.act(f1[:, 0:n], f1[:, 0:n], AF.Exp, [f1r], [f1r], scale=-0.5)
        for c in range(KC):
            if inplace:
                self.stt(xres[:, c, 0:n], xres[:, c, 0:n], vecs[:, gcol + c:gcol + c + 1], f1[:, 0:n], ALU.mult, ALU.mult,
                         [self.xres_r[c], self.vecs_r, f1r], [self.xres_r[c]])
            else:
                self.stt(self.h[:, c, 0:n], xres[:, c, 0:n], vecs[:, gcol + c:gcol + c + 1], f1[:, 0:n], ALU.mult, ALU.mult,
                         [self.xres_r[c], self.vecs_r, f1r], [self.h_r[c]])

    def proj(self, lhs_list, rhs_list, evac):
        n = self.n
        for oc, lk in enumerate(lhs_list):
            bank, br = self.psum()
            m = len(lk)
            for k in range(m):
                self.mm(bank[:, 0:n], lk[k][0], rhs_list[k][0], k == 0, k == m - 1, [lk[k][1], rhs_list[k][1]], [br])
            evac(oc, bank, br)

    def out_proj(self, wkey, k0):
        n = self.n
        U, xres = self.U, self.xres
        gk = [(U[:, k0 + k, 0:n], self.U_r[k0 + k]) for k in range(KC)]
        for p in range(2):
            w, wr = self.wnext(wkey, p)
            wv = w[:, :].rearrange("p (k n) -> p k n", k=KC)
            lhs = [[(wv[:, k, oc * 128:(oc + 1) * 128], wr) for k in range(KC)] for oc in range(4)]

            def ev_o(oc, bank, br, p=p):
                c = p * 4 + oc
                self.tt("dve", xres[:, c, 0:n], bank[:, 0:n], xres[:, c, 0:n], ALU.add, [br, self.xres_r[c]], [self.xres_r[c]])
            self.proj(lhs, gk, ev_o)

    def attention(self, ti, last, first_mask=None):
        h, U, kT, va, btab, der = self.h, self.U, self.kT, self.va, self.btab, self.der
        hk = [(h[:, k, :], self.h_r[k]) for k in range(KC)]
        for p in range(2):
            w, wr = self.wnext("qkv", p)
            wv = w[:, :].rearrange("p (k n) -> p k n", k=KC)
            lhs = [[(wv[:, k, oc * 128:(oc + 1) * 128], wr) for k in range(KC)] for oc in range(4)]

            def ev_q(oc, bank, br, p=p):
                m = p * 4 + oc
                self.act(U[:, m, :], bank[:], AF.Copy, [br], [self.U_r[m]], scale=0.125)
            self.proj(lhs, hk, ev_q)
        w, wr = self.wnext("qkv", 2)
        wv = w[:, :].rearrange("p (k n) -> p k n", k=KC)
        lhs = [[(wv[:, k, cc * 128:(cc + 1) * 128], wr) for k in range(KC)] for cc in range(2)]

        def ev_k(cc, bank, br):
            self.cp("act", kT[:, cc, 128:128 + T], bank[:], [br], [self.kT_r[cc]])
        self.proj(lhs, hk, ev_k)
        for b in range(NQB):
            bank, br = self.psum()
            for k in range(KC):
                self.mm(bank[:, 0:256], h[:, k, b * 128:(b + 1) * 128], wv[:, k, 256:512], k == 0, k == KC - 1, [self.h_r[k], wr], [br])
            for j in range(NKV):
                vo = (j % 2) * 64
                self.cp("dve" if j % 2 == 0 else "act", va[:, b + 1, j, vo:vo + 64], bank[:, j * 64:(j + 1) * 64], [br], [self.va_r[b + 1]])
            if last and b == NQB - 1:
                self.cp("act", self.kv32[:, 256:512], bank[:, 0:256], [br], [self.kv32b_r])
                bank2, br2 = self.psum()
                for k in range(KC):
                    self.mm(bank2[:, 0:256], h[:, k, b * 128:(b + 1) * 128], wv[:, k, 0:256], k == 0, k == KC - 1, [self.h_r[k], wr], [br2])
                self.cp("act", self.kv32[:, 0:256], bank2[:, 0:256], [br2], [self.kv32_r])
                self.out_evs.append(self.dma("sp", self.nk, self.kv32[:, 0:256], [self.kv32_r], []))
                self.out_evs.append(self.dma("sp", self.nv, self.kv32[:, 256:512], [self.kv32b_r], []))
        items = [(qb, j) for qb in range(NQB) for j in range(NKV)]
        pts = {}

        def scores(qb, j):
            Pp, half = j // 2, j % 2
            rows = slice(half * 64, half * 64 + 64)
            res = []
            for typ in range(2):
                bank, br = self.psum()
                kcol = qb * 128 + typ * 128
                kres = self.kTh_r[Pp] if (qb == 0 and typ == 0) else self.kT_r[Pp]
                self.mm(bank[:], kT[rows, Pp, kcol:kcol + 128], U[rows, 4 * Pp:4 * Pp + 4, qb * 128:(qb + 1) * 128], True, True,
                        [kres] + [self.U_r[4 * Pp + g] for g in range(4)], [br])
                f, fr = self.ftmp()
                self.tt("dve", f[:], bank[:], btab[:, typ * 4 + j, :], ALU.add, [br, self.btab_r], [fr])
                pi = 16 + (self._pt_i % 6)
                self._pt_i += 1
                if first_mask is not None and qb == 0 and typ == 0:
                    self.act(U[:, pi, :], f[:], AF.Exp, [fr, self.hb_r, self.cst_r], [self.U_r[pi]], bias=first_mask)
                else:
                    self.act(U[:, pi, :], f[:], AF.Exp, [fr], [self.U_r[pi]])
                res.append(pi)
            pts[(qb, j)] = res

        def pv(qb, j):
            Pp, half = j // 2, j % 2
            orow = slice(half * 64, half * 64 + 64)
            drow = slice((1 - half) * 64, (1 - half) * 64 + 64)
            bank, br = self.psum()
            for typ in range(2):
                pi = pts[(qb, j)][typ]
                self.mm(bank[:], va[:, qb + typ, j, :], U[:, pi, :], typ == 0, False, [self.va_r[qb + typ], self.U_r[pi]], [br])
            self.mm(bank[:], self.snk[:, j, :], self.ind[:, :], False, True, [self.snk_r], [br])
            f, fr = self.ftmp()
            fo = f[drow, :].rearrange("p (g q) -> p g q", g=4)
            self.act(f[drow, :], bank[drow, :], AF.Ln, [br], [fr])
            self.act(f[drow, :], f[drow, :], AF.Exp, [fr], [fr], scale=-1.0)
            outv = U[orow, 8 + 4 * Pp:8 + 4 * Pp + 4, qb * 128:(qb + 1) * 128]
            self.tt("dve", outv, bank[orow, :].rearrange("p (g q) -> p g q", g=4), fo, ALU.mult,
                    [br, fr], [self.U_r[8 + 4 * Pp + g] for g in range(4)])

        scores(*items[0])
        for i in range(len(items)):
            if i + 1 < len(items):
                scores(*items[i + 1])
            pv(*items[i])
        for cc in range(2):
            self.cp("pool", kT[:, cc, 0:128], kT[:, cc, T:T + 128], [self.kT_r[cc]], [self.kTh_r[cc]])
        self.cp("pool", va[:, 0, :, :], va[:, NQB, :, :], [self.va_r[NQB]], [self.va_r[0]])
        self.out_proj("wo", 8)

    def attention_s(self):
        n = NS
        h, U, der = self.h, self.U, self.der
        q32, kn32, kv32, bts, ones32 = self.q32, self.kn32, self.kv32, self.bts, self.ones32
        Kc = self.btab[:].rearrange("p a b -> p (a b)").rearrange("p (s q k) -> p s q k", s=SB, q=2)
        Vc = self.xcT[:].rearrange("p c t -> p (c t)").rearrange("p (s j d) -> p s j d", s=SB, j=NKV)
        vn4 = self.trT[0:4, :, :].rearrange("p c t -> p (c t)").rearrange("p (s c) -> p s c", s=SB)
        self.dma("sp", self.btab[:].rearrange("p a b -> p (a b)"), self.ckT_d, [], [self.btab_r])
        self.dma("sp", Vc.rearrange("p s j d -> p s (j d)"), self.cv_d.rearrange("s p c -> p s c"), [], list(self.xc_r))
        hk = [(h[:, k, 0:n], self.h_r[k]) for k in range(KC)]
        for p in range(2):
            w, wr = self.wnext("qkv", p)
            wv = w[:, :].rearrange("p (k n) -> p k n", k=KC)
            lhs = [[(wv[:, k, oc * 128:(oc + 1) * 128], wr) for k in range(KC)] for oc in range(4)]

            def ev_q(oc, bank, br, p=p):
                m = p * 4 + oc
                self.act(q32[:, m, :], bank[:, 0:n], AF.Copy, [br], [self.q32_r], scale=0.125)
            self.proj(lhs, hk, ev_q)
        w, wr = self.wnext("qkv", 2)
        wv = w[:, :].rearrange("p (k n) -> p k n", k=KC)
        lhs = [[(wv[:, k, cc * 128:(cc + 1) * 128], wr) for k in range(KC)] for cc in range(2)]

        def ev_k(cc, bank, br):
            self.cp("act", kn32[:, cc, :], bank[:, 0:n], [br], [self.kn32_r])
        self.proj(lhs, hk, ev_k)
        for which, c0, rr in ((0, 0, self.kv32_r), (1, 256, self.kv32b_r)):
            bank, br = self.psum()
            for k in range(KC):
                self.mm(bank[0:n, 0:256], h[:, k, 0:n], wv[:, k, c0:c0 + 256], k == 0, k == KC - 1, [self.h_r[k], wr], [br])
            self.cp("act" if which == 0 else "dve", kv32[0:n, c0:c0 + 256], bank[0:n, 0:256], [br], [rr])
        kscr_r, vscr_r = Res("kscr"), Res("vscr")
        self.dma("sp", self.kscr, kv32[0:n, 0:256], [self.kv32_r], [kscr_r])
        self.dma("sp", self.vscr, kv32[0:n, 256:512], [self.kv32b_r], [vscr_r])
        self.out_evs.append(self.dma("sp", self.nks[:, 124:128, :], self.kscr.rearrange("(s t) c -> s t c", t=DEC_T), [kscr_r], [Res("nks_n")]))
        self.out_evs.append(self.dma("sp", self.nvs[:, 124:128, :], self.vscr.rearrange("(s t) c -> s t c", t=DEC_T), [vscr_r], [Res("nvs_n")]))
        self.dma("sp", vn4, self.vscr.rearrange("(s t) c -> t s c", t=DEC_T), [vscr_r], list(self.tr_r))
        SC = [self.psum(), self.psum()]
        SN = [self.psum(), self.psum()]
        for s in range(SB):
            hb_, sl = s // 8, s % 8
            for j in range(NKV):
                Pp, half = j // 2, j % 2
                rows = slice(half * 64, half * 64 + 64)
                col = sl * 64 + j * 16
                rhs = q32[rows, 4 * Pp:4 * Pp + 4, 4 * s:4 * s + 4]
                self.mm(SC[hb_][0][:, col:col + 16], Kc[rows, s, Pp, :], rhs, True, True, [self.btab_r, self.q32_r], [SC[hb_][1]])
                self.mm(SN[hb_][0][0:4, col:col + 16], kn32[rows, Pp, 4 * s:4 * s + 4], rhs, True, True, [self.kn32_r, self.q32_r], [SN[hb_][1]])
        PC, PN = [], []
        for hb_ in range(2):
            f, fr = self.ftmp()
            self.tt("dve", f[:, :].rearrange("p (s c) -> p s c", s=8), SC[hb_][0][:, :].rearrange("p (s c) -> p s c", s=8),
                    bts[:, 0:64].unsqueeze(1).to_broadcast([128, 8, 64]), ALU.add, [SC[hb_][1], self.bts_r], [fr])
            self.act(f[:, :], f[:, :], AF.Exp, [fr], [fr])
            PC.append((f, fr))
            g_, gr = self.ftmp()
            self.tt("dve", g_[0:4, :].rearrange("p (s c) -> p s c", s=8), SN[hb_][0][0:4, :].rearrange("p (s c) -> p s c", s=8),
                    bts[0:4, 64:128].unsqueeze(1).to_broadcast([4, 8, 64]), ALU.add, [SN[hb_][1], self.bts_r], [gr])
            self.act(g_[0:4, :], g_[0:4, :], AF.Exp, [gr], [gr])
            PN.append((g_, gr))
        OB = [self.psum(), self.psum()]
        DB = [self.psum(), self.psum()]
        for s in range(SB):
            hb_, sl = s // 8, s % 8
            pc, pcr = PC[hb_]
            pn, pnr = PN[hb_]
            for j in range(NKV):
                col = sl * 64 + j * 16
                self.mm(OB[hb_][0][0:64, col:col + 16], Vc[:, s, j, :], pc[:, col:col + 16], True, False, list(self.xc_r) + [pcr], [OB[hb_][1]])
                self.mm(OB[hb_][0][0:64, col:col + 16], vn4[0:4, s, j * 64:(j + 1) * 64], pn[0:4, col:col + 16], False, True,
                        list(self.tr_r) + [pnr], [OB[hb_][1]])
                self.mm(DB[hb_][0][0:64, col:col + 16], ones32[:, :], pc[:, col:col + 16], True, False, [self.ones32_r, pcr], [DB[hb_][1]])
                self.mm(DB[hb_][0][0:64, col:col + 16], ones32[0:4, :], pn[0:4, col:col + 16], False, True, [self.ones32_r, pnr], [DB[hb_][1]])
        for hb_ in range(2):
            for j in range(NKV):
                Pp, half = j // 2, j % 2
                orow = slice(half * 64, half * 64 + 64)
                f, fr = self.ftmp()
                fv = f[orow, 0:128].rearrange("p (s g t) -> p s g t", s=8, g=4)
                dv = DB[hb_][0][0:64, :].rearrange("p (s j c) -> p s j c", s=8, j=NKV)[:, :, j, :].rearrange("p s (g t) -> p s g t", g=4)
                ov = OB[hb_][0][0:64, :].rearrange("p (s j c) -> p s j c", s=8, j=NKV)[:, :, j, :].rearrange("p s (g t) -> p s g t", g=4)
                esb = der[orow, 24 + 4 * j:24 + 4 * j + 4].unsqueeze(1).unsqueeze(3).to_broadcast([64, 8, 4, DEC_T])
                self.tt("dve", fv, dv, esb, ALU.add, [DB[hb_][1], self.der_r], [fr])
                self.op("dve", (lambda f, orow: lambda e: e.reciprocal(out=f[orow, 0:128], in_=f[orow, 0:128]))(f, orow), [fr], [fr])
                outv = U[orow, 8 + 4 * Pp:8 + 4 * Pp + 4, hb_ * 32:(hb_ + 1) * 32].rearrange("p g (s t) -> p s g t", t=DEC_T)
                self.tt("dve", outv, ov, fv, ALU.mult, [OB[hb_][1], fr], [self.U_r[8 + 4 * Pp + g] for g in range(4)])
        self.out_proj("wo", 8)

    def ffn(self, layer):
        n = self.n
        h, U, xres = self.h, self.U, self.xres
        kin, kout = "fi%d" % layer, "fo%d" % layer
        for p in range(11):
            w, wr = self.wnext(kin, p)
            wv = w[:, :].rearrange("p (k n) -> p k n", k=KC)
            for cl in range(2):
                c = 2 * p + cl
                bg, bgr = self.psum()
                bu, bur = self.psum()
                for k in range(KC):
                    self.mm(bg[:, 0:n], wv[:, k, cl * 128:(cl + 1) * 128], h[:, k, 0:n], k == 0, k == KC - 1, [wr, self.h_r[k]], [bgr])
                for k in range(KC):
                    self.mm(bu[:, 0:n], wv[:, k, 256 + cl * 128:256 + (cl + 1) * 128], h[:, k, 0:n], k == 0, k == KC - 1, [wr, self.h_r[k]], [bur])
                f, fr = self.ftmp()
                self.act(f[:, 0:n], bg[:, 0:n], AF.Silu, [bgr], [fr])
                self.tt("dve", U[:, c, 0:n], bu[:, 0:n], f[:, 0:n], ALU.mult, [bur, fr], [self.U_r[c]])
        for oc in range(KC):
            w, wr = self.wnext(kout, oc)
            wv = w[:, 0:FC * 128].rearrange("p (k n) -> p k n", k=FC)
            bank, br = self.psum()
            for k in range(FC):
                self.mm(bank[:, 0:n], wv[:, k, :], U[:, k, 0:n], k == 0, k == FC - 1, [wr, self.U_r[k]], [br])
            self.tt("dve", xres[:, oc, 0:n], bank[:, 0:n], xres[:, oc, 0:n], ALU.add, [br, self.xres_r[oc]], [self.xres_r[oc]])

    def recurrent(self, smp, state_only=False):
        n = self.n
        h, U, xres, xbt, vecs, der, cst = self.h, self.U, self.xres, self.xbt, self.vecs, self.der, self.cst
        hk = [(h[:, k, 0:n], self.h_r[k]) for k in range(KC)]

        def v3(ap):
            return ap.rearrange("p (s t) -> p s t", t=DEC_T)
        if smp:
            f, fr = self.ftmp()
            self.dma("sp", f[:, 0:KC * SB * 3], self.sconv_d, [], [fr])
            self.cp("pool", self.xps[:, :, :, 0:3], f[:, 0:KC * SB * 3].rearrange("p (c s r) -> p c s r", c=KC, s=SB), [fr], [self.xpsh_r])
        for p in ((2, 3) if state_only else range(4)):
            w, wr = self.wnext("ri", p)
            wv = w[:, :].rearrange("p (k n) -> p k n", k=KC)
            lhs = [[(wv[:, k, oc * 128:(oc + 1) * 128], wr) for k in range(KC)] for oc in range(4)]

            def ev_i(oc, bank, br, p=p):
                c = (p % 2) * 4 + oc
                if p < 2:
                    self.act(U[:, c, 0:n], bank[:, 0:n], AF.Gelu_apprx_tanh, [br], [self.U_r[c]])
                elif smp:
                    self.cp("dve", self.xps[:, c, :, 3:7], v3(bank[:, 0:n]), [br], [self.xps_r[c]])
                else:
                    self.cp("dve", xbt[:, c, 3:3 + T], bank[:], [br], [self.xbt_r[c]])
            self.proj(lhs, hk, ev_i)
            if smp and p >= 2:
                bank, br = self.psum()
                for k in range(KC):
                    self.mm(bank[0:n, :], h[:, k, 0:n], wv[:, k, :], k == 0, k == KC - 1, [self.h_r[k], wr], [br])
                self.cp("act", self.yout[0:n, 0, (p - 2) * 512:(p - 1) * 512], bank[0:n, :], [br], [self.yout_r[0]])
        if smp:
            xbs_r = Res("xbscr")
            self.dma("sp", self.xbscr, self.yout[0:n, 0, :], [self.yout_r[0]], [xbs_r])
            self.out_evs.append(self.dma("sp", self.ncs, self.xbscr.rearrange("(s t) c -> s t c", t=DEC_T)[:, 1:4, :], [xbs_r], [Res("ncs")]))
        for c in range(KC):
            xcc, xcr = self.xc[c], self.xc_r[c]
            if smp:
                rd = [self.xps_r[c], self.xpsh_r, self.vecs_r]
                xo = v3(xcc[:, 0:n])
                tap = lambda j, c=c: self.xps[:, c, :, j:j + DEC_T]
            else:
                rd = [self.xbt_r[c], self.xbh_r[c], self.vecs_r]
                xo = xcc[:, 0:n]
                tap = lambda j, c=c: xbt[:, c, j:j + T]
            self.act(xo, tap(3), AF.Identity, rd, [xcr], scale=vecs[:, V_CW + 24 + c:V_CW + 25 + c], bias=vecs[:, V_CB + c:V_CB + c + 1])
            for j in (2, 1, 0):
                self.stt(xo, tap(j), vecs[:, V_CW + 8 * j + c:V_CW + 8 * j + c + 1], xo, ALU.mult, ALU.add, rd + [xcr], [xcr])
            self.cp("act", U[:, 8 + c, 0:n], xcc[:, 0:n], [xcr], [self.U_r[8 + c]])
            if not smp:
                self.cp("pool", xbt[:, c, 0:3], xbt[:, c, T:T + 3], [self.xbt_r[c]], [self.xbh_r[c]])
        w, wr = self.wnext("gt", 0)
        wg = w[:, :].rearrange("p (g b k n) -> p g b k n", g=2, b=4, k=2)
        for oc in range(KC):
            blk, hf = oc // 2, oc % 2
            for gi in range(2):
                bank, br = self.psum()
                for k in range(2):
                    self.mm(bank[:, 0:n], wg[:, gi, blk, k, hf * 128:(hf + 1) * 128], U[:, 8 + 2 * blk + k, 0:n], k == 0, k == 1,
                            [wr, self.U_r[8 + 2 * blk + k]], [br])
                if gi == 0:
                    self.act(self.tr[oc][:, 0:n], bank[:, 0:n], AF.Tanh, [br, self.der_r], [self.tr_r[oc]], scale=0.5, bias=der[:, 8 + oc:9 + oc])
                else:
                    self.act(U[:, 16 + oc, 0:n], bank[:, 0:n], AF.Tanh, [br, self.der_r], [self.U_r[16 + oc]], scale=0.5, bias=der[:, 16 + oc:17 + oc])
                    xcc_, xcr_ = self.xc[oc], self.xc_r[oc]
                    self.stt(xcc_[:, 0:n], U[:, 16 + oc, 0:n], 1.0, xcc_[:, 0:n], ALU.add, ALU.mult, [self.U_r[16 + oc], xcr_], [xcr_])
        for oc in range(KC):
            tr, trr = self.tr[oc], self.tr_r[oc]
            self.act(tr[:, 0:n], tr[:, 0:n], AF.Exp, [trr, self.der_r], [trr], scale=der[:, oc:oc + 1], bias=der[:, oc:oc + 1])
        for oc in range(KC):
            tr, trr = self.tr[oc], self.tr_r[oc]
            xcc, xcr = self.xc[oc], self.xc_r[oc]
            s, sr = self.ftmp()
            self.tt("pool", s[:, 0:n], tr[:, 0:n], tr[:, 0:n], ALU.mult, [trr], [sr])
            self.act(s[:, 0:n], s[:, 0:n], AF.Ln, [sr, self.cst_r], [sr], scale=-1.0, bias=cst[:, 1:2])
            self.act(s[:, 0:n], s[:, 0:n], AF.Exp, [sr], [sr], scale=0.5)
            self.stt(xcc[:, 0:n], xcc[:, 0:n], 0.5, s[:, 0:n], ALU.mult, ALU.mult, [xcr, sr], [xcr])
            hs, hsr = self.ftmp()
            if smp:
                a3, b3 = v3(tr[:, 0:n]), v3(xcc[:, 0:n])
                t2, t2r = self.ftmp()
                self.tt("dve", t2[:, 0:SB], a3[:, :, 0], self.h0s[:, oc, :], ALU.mult, [trr, self.h0s_r], [t2r])
                self.tt("dve", b3[:, :, 0], b3[:, :, 0], t2[:, 0:SB], ALU.add, [xcr, t2r], [xcr])
                self.op("pool", (lambda a3: lambda e: e.memset(a3[:, :, 0:1], 0.0))(a3), [trr, t2r], [trr])
                self.op("dve", (lambda hs, tr, xcc: lambda e: e.tensor_tensor_scan(out=hs[:, 0:n], data0=tr[:, 0:n], data1=xcc[:, 0:n], initial=0.0,
                                                                                   op0=ALU.mult, op1=ALU.add))(hs, tr, xcc),
                        [trr, xcr], [hsr])
                self.cp("pool", self.hl[:, oc, :], v3(hs[:, 0:n])[:, :, DEC_T - 1], [hsr], [self.hl_r[oc]])
            else:
                self.op("dve", (lambda hs, tr, xcc, oc: lambda e: e.tensor_tensor_scan(out=hs[:], data0=tr[:], data1=xcc[:], initial=self.hcar[:, oc:oc + 1],
                                                                                       op0=ALU.mult, op1=ALU.add))(hs, tr, xcc, oc),
                        [trr, xcr, self.hcar_r[oc]], [hsr])
                self.cp("dve", self.hcar[:, oc:oc + 1], hs[:, T - 1:T], [hsr], [self.hcar_r[oc]])
            if not state_only:
                self.tt("pool", U[:, 16 + oc, 0:n], hs[:, 0:n], U[:, oc, 0:n], ALU.mult, [hsr, self.U_r[oc]], [self.U_r[16 + oc]])
        if smp:
            for half in range(2):
                bank, br = self.psum()
                for cc in range(4):
                    c = half * 4 + cc
                    self.op("pe", (lambda bank, cc, c: lambda e: e.transpose(bank[0:SB, cc * 128:(cc + 1) * 128], self.hl[:, c, :], self.ident[:]))(bank, cc, c),
                            [self.hl_r[c], self.ident_r], [br])
                self.cp("act", self.yout[0:SB, 1, half * 512:(half + 1) * 512], bank[0:SB, :], [br], [self.yout_r[1]])
            self.out_evs.append(self.dma("sp", self.nhs, self.yout[0:SB, 1, :], [self.yout_r[1]], []))
        if not state_only:
            self.out_proj("ro", 16)

    def store_y(self, ti):
        xres, yout, ident = self.xres, self.yout, self.ident
        for b in range(NQB):
            s = b % 2
            for half in range(2):
                bank, br = self.psum()
                for cc in range(4):
                    c = half * 4 + cc
                    self.op("pe", (lambda bank, cc, c, b: lambda e: e.transpose(bank[:, cc * 128:(cc + 1) * 128], xres[:, c, b * 128:(b + 1) * 128], ident[:]))(bank, cc, c, b),
                            [self.xres_r[c], self.ident_r], [br])
                eng = "act" if half == 0 else "dve"
                self.cp(eng, yout[:, s, half * 512:(half + 1) * 512], bank[:], [br], [self.yout_r[s]])
            self.out_evs.append(self.dma("sp", self.yp[ti * T + b * 128: ti * T + (b + 1) * 128, :], yout[:, s, :], [self.yout_r[s]], []))

    def store_ys(self):
        xres, yout, ident = self.xres, self.yout, self.ident
        for half in range(2):
            bank, br = self.psum()
            for cc in range(4):
                c = half * 4 + cc
                self.op("pe", (lambda bank, cc, c: lambda e: e.transpose(bank[0:NS, cc * 128:(cc + 1) * 128], xres[:, c, 0:NS], ident[:]))(bank, cc, c),
                        [self.xres_r[c], self.ident_r], [br])
            self.cp("act" if half == 0 else "dve", yout[0:NS, 0, half * 512:(half + 1) * 512], bank[0:NS, :], [br], [self.yout_r[0]])
        self.out_evs.append(self.dma("sp", self.ys, yout[0:NS, 0, :], [self.yout_r[0]], []))

    def store_state(self):
        for c in range(KC):
            self.out_evs.append(self.P.op("sp", (lambda c: lambda e: e.dma_start(
                out=self.ncv[:, c * 128:(c + 1) * 128].rearrange("r p -> p r"), in_=self.xbt[:, c, 0:3], allow_slow_non_contiguous=True))(c),
                [self.xbh_r[c]], [], dma=True))
        self.out_evs.append(self.P.op("sp", lambda e: e.dma_start(out=self.nh.rearrange("(c p) -> p c", p=128), in_=self.hcar[:, :], allow_slow_non_contiguous=True),
                                      self.hcar_r, [], dma=True))

    def build(self):
        self.declare()
        self.setup_consts()
        self.setup_weights()
        nt = self.n_tiles
        cst = self.cst
        for ti in range(nt):
            self.n = T
            self.load_x(ti, self.xpre)
            self.rmsnorm(V_AN)
            self.attention(ti, False, cst[:, 3:4] if ti == 0 else None)
            self.rmsnorm(V_FN0)
            self.ffn(0)
            self.rmsnorm(V_RN)
            self.recurrent(False, state_only=True)
        for c in range(KC):
            self.stt(self.hcar[:, c:c + 1], self.hcar[:, c:c + 1], self.flag[:, 0:1], self.hcar[:, c:c + 1], ALU.mult, ALU.bypass,
                     [self.hcar_r[c], self.flag_r], [self.hcar_r[c]]) if False else \
                self.op("dve", (lambda c: lambda e: e.tensor_scalar(out=self.hcar[:, c:c + 1], in0=self.hcar[:, c:c + 1], scalar1=self.flag[:, 0:1],
                                                                    scalar2=None, op0=ALU.mult))(c), [self.hcar_r[c], self.flag_r], [self.hcar_r[c]])
            self.op("dve", (lambda c: lambda e: e.tensor_scalar(out=self.xbt[:, c, 0:3], in0=self.xbt[:, c, 0:3], scalar1=self.flag[:, 0:1],
                                                                scalar2=None, op0=ALU.mult))(c), [self.xbh_r[c], self.flag_r], [self.xbh_r[c]])
        for ti in range(nt):
            last = ti == nt - 1
            self.n = T
            self.load_x(ti)
            self.rmsnorm(V_AN)
            self.attention(ti, last, self.hb[:, 0:1] if ti == 0 else None)
            self.rmsnorm(V_FN0)
            self.ffn(0)
            self.rmsnorm(V_RN)
            self.recurrent(False)
            self.rmsnorm(V_FN1)
            self.ffn(1)
            self.rmsnorm(V_FIN, inplace=True)
            self.store_y(ti)
        self.store_state()
        if self.sample:
            self.n = NS
            self.load_xs()
            self.rmsnorm(V_AN)
            self.attention_s()
            self.rmsnorm(V_FN0)
            self.ffn(0)
            self.rmsnorm(V_RN)
            self.recurrent(True)
            self.rmsnorm(V_FN1)
            self.ffn(1)
            self.rmsnorm(V_FIN, inplace=True)
            self.store_ys()
        self.P.final_wait("sp", self.out_evs)
        self.P.emit()
        self.st.close()
        return self.nc


def _kmajor(w, ncols_piece):
    K, N = w.shape
    kc = K // 128
    npc = N // ncols_piece
    a = w.reshape(kc, 128, npc, ncols_piece).transpose(1, 2, 0, 3)
    return np.ascontiguousarray(a).reshape(128, npc * kc * ncols_piece)


def _head_perm():
    cols = []
    for m in range(8):
        Pp, g = m // 4, m % 4
        a = 8 * Pp + g
        b = 8 * Pp + 4 + g
        cols += list(range(a * 64, a * 64 + 64)) + list(range(b * 64, b * 64 + 64))
    return np.array(cols)


def prep_weights(inp):
    f = np.float32
    perm = _head_perm()
    wqkv = np.asarray(inp["w_qkv"][0], f)
    wq = wqkv[:, :1024][:, perm]
    wkv = wqkv[:, 1024:1536]
    out = {}
    out["w_qkv"] = _kmajor(np.concatenate([wq, wkv], axis=1), 512)
    out["w_wo"] = _kmajor(np.asarray(inp["w_attn_out"][0], f)[perm, :], 512)
    for l in range(2):
        wi = np.asarray(inp["w_ffn_in"][l], f)
        g, u = wi[:, :DFF], wi[:, DFF:]
        cols = []
        for p in range(11):
            cols.append(g[:, p * 256:(p + 1) * 256])
            cols.append(u[:, p * 256:(p + 1) * 256])
        out["w_fi%d" % l] = _kmajor(np.concatenate(cols, axis=1), 512)
        out["w_fo%d" % l] = _kmajor(np.asarray(inp["w_ffn_out"][l], f), 128)
    out["w_ri"] = _kmajor(np.asarray(inp["w_rec_in"][0], f), 512)
    wg = np.stack([np.asarray(inp["w_rgate"][0], f), np.asarray(inp["w_igate"][0], f)])
    wg = wg.reshape(2, 4, 2, 128, 256).transpose(3, 0, 1, 2, 4)
    out["w_gt"] = np.ascontiguousarray(wg).reshape(128, 4096)
    out["w_ro"] = _kmajor(np.asarray(inp["w_rec_out"][0], f), 512)
    return out


def prep_vecs(inp):
    f = np.float32
    v = np.zeros((128, NV), f)

    def colmaj(x):
        return np.asarray(x, f).reshape(8, 128).T
    v[:, V_AN:V_AN + 8] = colmaj(inp["attn_norm"][0])
    v[:, V_FN0:V_FN0 + 8] = colmaj(inp["ffn_norm"][0])
    v[:, V_RN:V_RN + 8] = colmaj(inp["rec_norm"][0])
    v[:, V_FN1:V_FN1 + 8] = colmaj(inp["ffn_norm"][1])
    v[:, V_FIN:V_FIN + 8] = colmaj(inp["final_norm"])
    for j in range(4):
        v[:, V_CW + 8 * j:V_CW + 8 * j + 8] = colmaj(inp["conv_w"][0][j])
    v[:, V_CB:V_CB + 8] = colmaj(inp["conv_b"][0])
    v[:, V_BRG:V_BRG + 8] = colmaj(inp["b_rgate"][0])
    v[:, V_BIG:V_BIG + 8] = colmaj(inp["b_igate"][0])
    v[:, V_LAM:V_LAM + 8] = colmaj(inp["lru_lambda"][0])
    v[:, V_SINK:V_SINK + 16] = np.asarray(inp["attn_sinks"][0], f)[None, :]
    return v


def bias_table():
    slopes = (2.0 ** (-8.0 * np.arange(1, NH + 1, dtype=np.float64) / NH))
    key = np.arange(128)[:, None]
    q = np.arange(128)[None, :]
    tab = np.zeros((128, 2, NKV, 4, 128), np.float32)
    for typ in range(2):
        dist = q - key + (128 if typ == 0 else 0)
        valid = (dist >= 0) & (dist <= 128)
        for j in range(NKV):
            for g in range(4):
                hh = 4 * j + g
                tab[:, typ, j, g, :] = np.where(valid, -slopes[hh] * dist, NEG).astype(np.float32)
    return tab.reshape(128, 8 * 512)


_CACHE = {}


def bias_table_s():
    slopes = (2.0 ** (-8.0 * np.arange(1, NH + 1, dtype=np.float64) / NH))
    tab = np.zeros((128, 128), np.float32)
    pos = np.arange(128)[:, None]
    for j in range(NKV):
        for g in range(4):
            for t in range(DEC_T):
                col = j * 16 + g * 4 + t
                sl = slopes[4 * j + g]
                tab[:, col] = np.where(pos[:, 0] >= t, -sl * (t + 128 - pos[:, 0]), NEG)
                for nn in range(DEC_T):
                    tab[nn, 64 + col] = (-sl * (t - nn)) if nn <= t else NEG
    return tab


def run_all(inp, n_tiles, sample=True):
    key = ("p", n_tiles, sample)
    if key not in _CACHE:
        _CACHE[key] = Builder(n_tiles, sample).build()
    nc = _CACHE[key]
    f = np.float32
    wts = prep_weights(inp)
    vecs = prep_vecs(inp)
    bt = bias_table()
    bts = bias_table_s()
    L = n_tiles * T
    in_maps = []
    for c in range(8):
        b, hh = c // 2, c % 2
        m = dict(wts)
        m["vecs"] = vecs
        m["btab"] = bt
        m["hb"] = np.full((128, 1), NEG if hh == 0 else 0.0, f)
        m["flag"] = np.full((128, 1), float(hh), f)
        m["sinkT"] = np.ascontiguousarray(np.asarray(inp["attn_sinks"][0], f).reshape(4, 4).T)
        xp = np.asarray(inp["x_prompt"][b], f)
        m["xp"] = np.ascontiguousarray(xp[hh * L:(hh + 1) * L])
        m["xpre"] = np.ascontiguousarray(xp[0:L]) if hh == 1 else np.zeros((L, D), f)
        if sample:
            b0 = c * SB
            m["xs"] = np.ascontiguousarray(np.asarray(inp["x_sample"][b0:b0 + SB], f).reshape(NS, D))
            ck = np.asarray(inp["cache_k"][0, b0:b0 + SB], f)
            cv = np.asarray(inp["cache_v"][0, b0:b0 + SB], f)
            m["ck"] = np.ascontiguousarray(ck.reshape(SB, 128, 256))
            m["cv"] = np.ascontiguousarray(cv.reshape(SB, 128, 256))
            ckT = ck.reshape(SB, 128, 2, 2, HD).transpose(3, 4, 0, 2, 1)
            m["ckT"] = np.ascontiguousarray(ckT).reshape(128, SB * 2 * 128)
            sc = np.asarray(inp["state_conv"][0, b0:b0 + SB], f).reshape(SB, 3, KC, 128).transpose(3, 2, 0, 1)
            m["sconv"] = np.ascontiguousarray(sc).reshape(128, KC * SB * 3)
            sh = np.asarray(inp["state_h"][0, b0:b0 + SB], f).reshape(SB, KC, 128).transpose(2, 1, 0)
            m["sh0"] = np.ascontiguousarray(sh).reshape(128, KC * SB)
            m["bts"] = bts
        in_maps.append(m)
    res = run_bass_kernel_spmd(nc, in_maps, core_ids=list(range(8)))
    return res.results


def kernel(**inp):
    n_tiles = SEQ // T // 2
    r = run_all(inp, n_tiles, sample=True)
    f = np.float32
    y_prompt = np.stack([np.concatenate([r[2 * b]["yp"], r[2 * b + 1]["yp"]]) for b in range(4)]).astype(f)
    nk = np.stack([r[2 * b + 1]["nk"].reshape(128, 4, 64) for b in range(4)])[None].astype(f)
    nv = np.stack([r[2 * b + 1]["nv"].reshape(128, 4, 64) for b in range(4)])[None].astype(f)
    ncv = np.stack([r[2 * b + 1]["ncv"] for b in range(4)])[None].astype(f)
    nh = np.stack([r[2 * b + 1]["nh"] for b in range(4)])[None].astype(f)
    y_sample = np.concatenate([r[c]["ys"].reshape(SB, DEC_T, D) for c in range(8)]).astype(f)
    nks = np.concatenate([r[c]["nks"].reshape(SB, 128, 4, 64) for c in range(8)])[None].astype(f)
    nvs = np.concatenate([r[c]["nvs"].reshape(SB, 128, 4, 64) for c in range(8)])[None].astype(f)
    ncs = np.concatenate([r[c]["ncs"] for c in range(8)])[None].astype(f)
    nhs = np.concatenate([r[c]["nhs"] for c in range(8)])[None].astype(f)
    return (y_prompt, y_sample, nk, nv, nks, nvs, ncv, nh, ncs, nhs)
```

```python
import contextlib
import os
import numpy as np
import concourse.bass as bass
import concourse.mybir as mybir
from concourse.bass_utils import run_bass_kernel_spmd

F32 = mybir.dt.float32
BF16 = mybir.dt.bfloat16
AF = mybir.ActivationFunctionType
ALU = mybir.AluOpType

ENGS = ("pe", "act", "dve", "pool", "sp")


class Res:
    __slots__ = ("name", "w", "r", "lsem", "ssem")

    def __init__(self, name):
        self.name = name
        self.w = None
        self.r = []
        self.lsem = None
        self.ssem = None


class Ev:
    __slots__ = ("kind", "key", "val", "clock", "op")

    def __init__(self, kind, key, val, clock, op):
        self.kind = kind
        self.key = key
        self.val = val
        self.clock = clock
        self.op = op


class OpRec:
    __slots__ = ("eng", "fn", "waits", "marked", "idx", "dma", "semslot", "inc")

    def __init__(self, eng, fn, idx):
        self.eng = eng
        self.fn = fn
        self.waits = []
        self.marked = False
        self.idx = idx
        self.dma = False
        self.semslot = None
        self.inc = 16


class Prog:
    def __init__(self, nc):
        self.nc = nc
        self.ops = {e: [] for e in ENGS}
        self.clock = {e: {} for e in ENGS}
        self.dsem_count = []
        self.dsem_last = {}

    def _need(self, eng, ev):
        if ev.kind == 'c' and ev.key[1] == eng and eng == 'pe':
            return False
        return self.clock[eng].get(ev.key, -1) < ev.val

    def _merge(self, eng, ev):
        c = self.clock[eng]
        for k, v in ev.clock.items():
            if c.get(k, -1) < v:
                c[k] = v
        if c.get(ev.key, -1) < ev.val:
            c[ev.key] = ev.val

    def op(self, eng, fn, reads=(), writes=(), dma=False, inc=16):
        rec = OpRec(eng, fn, len(self.ops[eng]))
        rec.dma = dma
        rec.inc = inc
        deps = []
        for r in reads:
            if r.w is not None:
                deps.append(r.w)
        for w in writes:
            if w.w is not None:
                deps.append(w.w)
            deps.extend(w.r)
        deps.sort(key=lambda ev: -ev.val)
        for ev in deps:
            if self._need(eng, ev):
                rec.waits.append(ev)
                if ev.kind == 'c':
                    ev.op.marked = True
                self._merge(eng, ev)
        if dma:
            if writes:
                tgt = writes[0]
                if tgt.lsem is None:
                    self.dsem_count.append(0)
                    tgt.lsem = len(self.dsem_count) - 1
                slot = tgt.lsem
            else:
                tgt = reads[0]
                if tgt.ssem is None:
                    self.dsem_count.append(0)
                    tgt.ssem = len(self.dsem_count) - 1
                slot = tgt.ssem
            last = self.dsem_last.get(slot)
            if last is not None and self._need(eng, last):
                rec.waits.append(last)
                self._merge(eng, last)
            self.dsem_count[slot] += inc
            rec.semslot = slot
            ev = Ev('d', ('d', slot), self.dsem_count[slot], dict(self.clock[eng]), rec)
            self.dsem_last[slot] = ev
        else:
            ev = Ev('c', ('c', eng), rec.idx, dict(self.clock[eng]), rec)
        self.ops[eng].append(rec)
        for r in reads:
            r.r.append(ev)
        for w in writes:
            w.w = ev
            w.r = []
        return ev

    def final_wait(self, eng, evs):
        rec = OpRec(eng, None, len(self.ops[eng]))
        for ev in sorted(evs, key=lambda ev: -ev.val):
            if self._need(eng, ev):
                rec.waits.append(ev)
                if ev.kind == 'c':
                    ev.op.marked = True
                self._merge(eng, ev)
        self.ops[eng].append(rec)

    def emit(self):
        nc = self.nc
        with contextlib.ExitStack() as st:
            esem = {e: st.enter_context(nc.semaphore("s_" + e)) for e in ENGS}
            dsem = [st.enter_context(nc.semaphore("d%d" % i)) for i in range(len(self.dsem_count))]
            rank = {}
            for e in ENGS:
                n = 0
                for rec in self.ops[e]:
                    if rec.marked:
                        n += 1
                    rank[(e, rec.idx)] = n
            block = st.enter_context(nc.Block())

            def run(e, eng):
                for rec in self.ops[e]:
                    for ev in rec.waits:
                        if ev.kind == 'c':
                            eng.wait_ge(esem[ev.key[1]], rank[(ev.key[1], ev.val)])
                        else:
                            eng.wait_ge(dsem[ev.key[1]], ev.val)
                    if rec.fn is None:
                        continue
                    ins = rec.fn(eng)
                    if rec.dma:
                        ins.then_inc(dsem[rec.semslot], rec.inc)
                    elif rec.marked:
                        ins.then_inc(esem[e], 1)

            @block.tensor
            def _(eng):
                run("pe", eng)

            @block.scalar
            def _(eng):
                run("act", eng)

            @block.vector
            def _(eng):
                run("dve", eng)

            @block.gpsimd
            def _(eng):
                run("pool", eng)

            @block.sync
            def _(eng):
                run("sp", eng)


D = 1024
KC = 8
DFF = 2816
FC = 22
T = 512
NQB = 4
NH = 16
NKV = 4
HD = 64
SEQ = 8192
BATCH = 4
DEC_B = 128
DEC_T = 4
EPS = 1e-6
NEG = -30000.0
SLOT = 4096
NSLOT = 5
NFROT = 6
SB = 16
NS = 64
XW = 32

V_AN, V_FN0, V_RN, V_FN1, V_FIN = 0, 8, 16, 24, 32
V_CW, V_CB, V_BRG, V_BIG, V_LAM, V_SINK = 40, 72, 80, 88, 96, 104
NV = 120

W_ORDER = ["qkv", "wo", "fi0", "fo0", "ri", "gt", "ro", "fi1", "fo1"]
W_PIECES = {"qkv": 3, "wo": 2, "fi0": 11, "fo0": 8, "ri": 4, "gt": 1, "ro": 2, "fi1": 11, "fo1": 8}
W_PSZ = {"qkv": 4096, "wo": 4096, "fi0": 4096, "fo0": 2816, "ri": 4096, "gt": 4096, "ro": 4096, "fi1": 4096, "fo1": 2816}


class Builder:
    def __init__(self, n_tiles, sample=True):
        self.n_tiles = n_tiles
        self.sample = sample
        self.nc = bass.Bass("TRN2", target_bir_lowering=False)
        self.st = contextlib.ExitStack()
        self.P = Prog(self.nc)
        self._psum_i = 0
        self._f_i = 0
        self._sq_i = 0
        self._pt_i = 0
        self.out_evs = []
        self.n = T

    def dram(self, name, shape, dt=F32, kind="ExternalInput"):
        return self.nc.dram_tensor(name, list(shape), dt, kind=kind).ap()

    def sb(self, name, shape, dt=F32):
        return self.st.enter_context(self.nc.sbuf_tensor(name, list(shape), dt))

    def psum(self):
        i = self._psum_i % 8
        self._psum_i += 1
        return self.ps[i], self.ps_r[i]

    def ftmp(self):
        i = self._f_i % NFROT
        self._f_i += 1
        return self.F[i], self.F_r[i]

    def op(self, eng, fn, reads=(), writes=(), **kw):
        return self.P.op(eng, fn, reads, writes, **kw)

    def act(self, out, in_, func, reads, writes, **kw):
        return self.P.op("act", lambda e: e.activation(out=out, in_=in_, func=func, **kw), reads, writes)

    def mm(self, out, lhsT, rhs, start, stop, reads, writes):
        return self.P.op("pe", lambda e: e.matmul(out, lhsT=lhsT, rhs=rhs, start=start, stop=stop), reads, writes)

    def tt(self, eng, out, in0, in1, op, reads, writes):
        return self.P.op(eng, lambda e: e.tensor_tensor(out=out, in0=in0, in1=in1, op=op), reads, writes)

    def stt(self, out, in0, scalar, in1, op0, op1, reads, writes):
        return self.P.op("dve", lambda e: e.scalar_tensor_tensor(out=out, in0=in0, scalar=scalar, in1=in1, op0=op0, op1=op1), reads, writes)

    def cp(self, eng, out, in_, reads, writes):
        if eng == "act":
            return self.act(out, in_, AF.Copy, reads, writes)
        return self.P.op(eng, lambda e: e.tensor_copy(out=out, in_=in_), reads, writes)

    def dma(self, eng, out, in_, reads, writes, **kw):
        return self.P.op(eng, lambda e: e.dma_start(out=out, in_=in_, **kw), reads, writes, dma=True)

    def declare(self):
        nt = self.n_tiles
        self.xp = self.dram("xp", [nt * T, D])
        self.vecs_d = self.dram("vecs", [128, NV])
        self.btab_d = self.dram("btab", [128, 8 * 512])
        self.hb_d = self.dram("hb", [128, 1])
        self.xhalo_d = self.dram("xhalo", [128, D])
        self.sel_d = self.dram("sel", [128, 8])
        self.x1s = self.dram("x1s", [nt, 128, KC * T], kind="Internal")
        self.pqs = self.dram("pqs", [nt, 128, 16 * T], BF16, kind="Internal")
        self.bounce = self.dram("bounce", [128, XW], kind="Internal")
        self.gath = self.dram("gath", [8 * 128, XW], kind="Internal")
        self.w_in = {k: self.dram("w_" + k, [128, W_PIECES[k] * W_PSZ[k]]) for k in W_ORDER}
        self.w_sc = {k: self.dram("s_" + k, [128, W_PIECES[k] * W_PSZ[k]], BF16, kind="Internal") for k in W_ORDER}
        self.yp = self.dram("yp", [nt * T, D], kind="ExternalOutput")
        self.nk = self.dram("nk", [128, 256], kind="ExternalOutput")
        self.nv = self.dram("nv", [128, 256], kind="ExternalOutput")
        self.ncv = self.dram("ncv", [3, D], kind="ExternalOutput")
        self.nh = self.dram("nh", [D], kind="ExternalOutput")
        if self.sample:
            self.xs_d = self.dram("xs", [NS, D])
            self.ckT_d = self.dram("ckT", [128, SB * 2 * 128])
            self.ck_d = self.dram("ck", [SB, 128, 256])
            self.cv_d = self.dram("cv", [SB, 128, 256])
            self.sconv_d = self.dram("sconv", [128, KC * SB * 3])
            self.sh0_d = self.dram("sh0", [128, KC * SB])
            self.bts_d = self.dram("bts", [128, 128])
            self.ys = self.dram("ys", [NS, D], kind="ExternalOutput")
            self.nks = self.dram("nks", [SB, 128, 256], kind="ExternalOutput")
            self.nvs = self.dram("nvs", [SB, 128, 256], kind="ExternalOutput")
            self.ncs = self.dram("ncs", [SB, 3, D], kind="ExternalOutput")
            self.nhs = self.dram("nhs", [SB, D], kind="ExternalOutput")
            self.kscr = self.dram("kscr", [NS, 256], kind="Internal")
            self.vscr = self.dram("vscr", [NS, 256], kind="Internal")
            self.xbscr = self.dram("xbscr", [NS, D], kind="Internal")

        sb = self.sb
        self.vecs = sb("vecs_sb", [128, NV]); self.vecs_r = Res("vecs")
        self.der = sb("der_sb", [128, 48]); self.der_r = Res("der")
        self.btab = sb("btab_sb", [128, 8, 512]); self.btab_r = Res("btab")
        self.hb = sb("hb_sb", [128, 1]); self.hb_r = Res("hb")
        self.ident = sb("ident", [128, 128]); self.ident_r = Res("ident")
        self.ones = sb("ones_bf", [128, 128], BF16); self.ones_r = Res("ones")
        self.cst = sb("cst", [128, 4]); self.cst_r = Res("cst")
        self.xin = sb("xin", [128, 2, D]); self.xin_r = [Res("xin%d" % i) for i in range(2)]
        self.yout = sb("yout", [128, 2, D]); self.yout_r = [Res("yout%d" % i) for i in range(2)]
        self.xres = sb("xres", [128, KC, T]); self.xres_r = [Res("xres%d" % i) for i in range(KC)]
        self.h = sb("h", [128, KC, T], BF16); self.h_r = [Res("h%d" % i) for i in range(KC)]
        self.sq = sb("sq", [128, 3, T], BF16); self.sq_r = [Res("sq%d" % i) for i in range(3)]
        self.U = sb("U", [128, 24, T], BF16); self.U_r = [Res("U%d" % i) for i in range(24)]
        self.kT = sb("kT", [128, 2, 128 + T], BF16)
        self.kT_r = [Res("kT%d" % i) for i in range(2)]; self.kTh_r = [Res("kTh%d" % i) for i in range(2)]
        self.va = sb("vaug", [128, 5, NKV, 128], BF16); self.va_r = [Res("va%d" % i) for i in range(5)]
        self.xbt = sb("xbt", [128, KC, 3 + T]); self.xbt_r = [Res("xbt%d" % i) for i in range(KC)]
        self.xbh_r = [Res("xbh%d" % i) for i in range(KC)]
        self.hcar = sb("hcar", [128, KC]); self.hcar_r = [Res("hcar%d" % i) for i in range(KC)]
        self.acar = sb("acar", [128, KC]); self.acar_r = [Res("acar%d" % i) for i in range(KC)]
        self.sm = sb("sm", [128, 512]); self.sm_r = Res("sm")
        self.sel = sb("sel_sb", [128, 8]); self.sel_r = Res("sel")
        self.G = sb("G_sb", [128, 8, XW]); self.G_r = Res("G")
        self.h2 = sb("h2", [128, KC]); self.h2_r = Res("h2")
        self.F = [sb("F%d" % i, [128, T]) for i in range(NFROT)]; self.F_r = [Res("F%d" % i) for i in range(NFROT)]
        self.trT = sb("trT", [128, KC, T]); self.tr_r = [Res("tr%d" % i) for i in range(KC)]
        self.xcT = sb("xcT", [128, KC, T]); self.xc_r = [Res("xc%d" % i) for i in range(KC)]
        self.tr = [self.trT[:, i, :] for i in range(KC)]
        self.xc = [self.xcT[:, i, :] for i in range(KC)]
        self.kv32 = sb("kv32", [128, 512]); self.kv32_r = Res("kv32"); self.kv32b_r = Res("kv32b")
        self.wslot = [sb("wslot%d" % i, [128, SLOT], BF16) for i in range(NSLOT)]
        self.wslot_r = [Res("wslot%d" % i) for i in range(NSLOT)]
        if self.sample:
            self.q32 = sb("q32_sb", [128, KC, NS]); self.q32_r = Res("q32")
            self.kn32 = sb("kn32_sb", [128, 2, NS]); self.kn32_r = Res("kn32")
            self.xps = sb("xps_sb", [128, KC, SB, 7]); self.xps_r = [Res("xps%d" % i) for i in range(KC)]; self.xpsh_r = Res("xpsh")
            self.h0s = sb("h0s_sb", [128, KC, SB]); self.h0s_r = Res("h0s")
            self.hl = sb("hl_sb", [128, KC, SB]); self.hl_r = [Res("hl%d" % i) for i in range(KC)]
            self.bts = sb("bts_sb", [128, 128]); self.bts_r = Res("bts")
            self.ones32 = sb("ones32", [128, 64]); self.ones32_r = Res("ones32")
        self.ps = [self.st.enter_context(self.nc.psum_tensor("ps%d" % i, [128, 512], F32)) for i in range(8)]
        self.ps_r = [Res("ps%d" % i) for i in range(8)]

    def setup_weights(self):
        self.wchunk_r = {}
        stA = self.xres[:].rearrange("p c t -> p (c t)")
        stB = self.xcT[:].rearrange("p c t -> p (c t)")
        stage = [(stA, list(self.xres_r)), (stB, list(self.xc_r))]
        i = 0
        for k in W_ORDER:
            psz = W_PSZ[k]
            for pi in range(W_PIECES[k]):
                st_ap, st_res = stage[i % 2]
                s = i % NSLOT
                if pi == 0:
                    rk = Res("wc_%s" % k)
                r = rk
                self.dma("sp", st_ap[:, 0:psz], self.w_in[k][:, pi * psz:(pi + 1) * psz], [], st_res)
                a, b = (psz // 3) // 128 * 128, (2 * psz // 3) // 128 * 128
                for eng, c0, c1 in (("act", 0, a), ("dve", a, b), ("pool", b, psz)):
                    self.cp(eng, self.wslot[s][:, c0:c1], st_ap[:, c0:c1], st_res, [self.wslot_r[s]])
                self.dma("act", self.w_sc[k][:, pi * psz:(pi + 1) * psz], self.wslot[s][:, 0:psz], [self.wslot_r[s]], [r])
                self.wchunk_r[(k, pi)] = r
                i += 1
        self.wseq = []

        def addw(keys):
            for k in keys:
                for i in range(W_PIECES[k]):
                    self.wseq.append((k, i))
        for ti in range(self.n_tiles):
            addw(["qkv", "wo", "fi0", "fo0", "ri", "gt"])
        addw(["gt"])
        if self.sample:
            addw(W_ORDER)
        addw(["gt"])
        for ti in range(self.n_tiles):
            addw(["ro", "fi1", "fo1"])
        self.w_issued = 0
        self.w_used = 0

    def _issue_w(self):
        idx = self.w_issued
        k, i = self.wseq[idx]
        s = idx % NSLOT
        psz = W_PSZ[k]
        self.dma("sp", self.wslot[s][:, 0:psz], self.w_sc[k][:, i * psz:(i + 1) * psz],
                 [self.wchunk_r[(k, i)]], [self.wslot_r[s]])
        self.w_issued += 1

    def wnext(self, k, i):
        idx = self.w_used
        assert self.wseq[idx] == (k, i), (self.wseq[idx], k, i)
        while self.w_issued < min(len(self.wseq), idx + NSLOT):
            self._issue_w()
        self.w_used += 1
        s = idx % NSLOT
        return self.wslot[s], self.wslot_r[s]

    def setup_consts(self):
        self.dma("sp", self.vecs[:], self.vecs_d, [], [self.vecs_r])
        self.dma("sp", self.hb[:], self.hb_d, [], [self.hb_r])
        self.dma("sp", self.sel[:], self.sel_d, [], [self.sel_r])
        self.dma("sp", self.btab[:].rearrange("p a b -> p (a b)"), self.btab_d, [], [self.btab_r])
        op = self.op
        ident, ones, cst = self.ident, self.ones, self.cst
        op("pool", lambda e: e.memset(ident[:], 0.0), [], [self.ident_r])
        op("pool", lambda e: e.affine_select(out=ident[:], in_=ident[:], pattern=[[-1, 128]], compare_op=ALU.not_equal,
                                             fill=1.0, base=0, channel_multiplier=1), [self.ident_r], [self.ident_r])
        op("pool", lambda e: e.memset(ones[:], 1.0), [], [self.ones_r])
        op("pool", lambda e: e.memset(cst[:, 0:1], EPS), [], [self.cst_r])
        op("pool", lambda e: e.memset(cst[:, 1:2], 1.0), [self.cst_r], [self.cst_r])
        op("pool", lambda e: e.memset(cst[:, 2:3], 0.0), [self.cst_r], [self.cst_r])
        kT, va, xbt, hcar = self.kT, self.va, self.xbt, self.hcar
        for cc in range(2):
            op("pool", (lambda cc: lambda e: e.memset(kT[:, cc, 0:128], 0.0))(cc), [], [self.kTh_r[cc]])
        for b in range(5):
            op("pool", (lambda b: lambda e: e.memset(va[:, b, :, :], 1.0 if b else 0.0))(b), [], [self.va_r[b]])
        op("pool", lambda e: e.memset(va[:, 0, 0::2, 64:128], 1.0), [self.va_r[0]], [self.va_r[0]])
        op("pool", lambda e: e.memset(va[:, 0, 1::2, 0:64], 1.0), [self.va_r[0]], [self.va_r[0]])
        for c in range(KC):
            op("pool", (lambda c: lambda e: e.memset(xbt[:, c, 0:3], 0.0))(c), [], [self.xbh_r[c]])
            op("pool", (lambda c: lambda e: e.memset(hcar[:, c:c + 1], 0.0))(c), [], [self.hcar_r[c]])
            op("pool", (lambda c: lambda e: e.memset(self.acar[:, c:c + 1], 1.0))(c), [], [self.acar_r[c]])
        der, vecs = self.der, self.vecs
        self.act(der[:, 0:8], vecs[:, V_LAM:V_LAM + 8], AF.Exp, [self.vecs_r], [self.der_r], scale=-1.0)
        self.act(der[:, 0:8], der[:, 0:8], AF.Ln, [self.der_r, self.cst_r], [self.der_r], bias=cst[:, 1:2])
        op("dve", lambda e: e.tensor_scalar(out=der[:, 0:8], in0=der[:, 0:8], scalar1=-4.0, scalar2=None, op0=ALU.mult),
           [self.der_r], [self.der_r])
        op("dve", lambda e: e.tensor_scalar(out=der[:, 8:24], in0=vecs[:, V_BRG:V_BRG + 16], scalar1=0.5, scalar2=None, op0=ALU.mult),
           [self.vecs_r, self.der_r], [self.der_r])
        self.act(der[:, 24:40], vecs[:, V_SINK:V_SINK + 16], AF.Exp, [self.vecs_r, self.der_r], [self.der_r])
        if self.sample:
            ones32 = self.ones32
            op("pool", lambda e: e.memset(ones32[:], 1.0), [], [self.ones32_r])
            self.dma("sp", self.bts[:], self.bts_d, [], [self.bts_r])
            self.dma("sp", self.h0s[:].rearrange("p c s -> p (c s)"), self.sh0_d, [], [self.h0s_r])
            self.cache_r = [Res("nks_c"), Res("nvs_c")]
            self.out_evs.append(self.dma("act", self.nks[:, 0:124, :], self.ck_d[:, 4:128, :], [], [self.cache_r[0]]))
            self.out_evs.append(self.dma("act", self.nvs[:, 0:124, :], self.cv_d[:, 4:128, :], [], [self.cache_r[1]]))

    def load_x(self, ti):
        xin = self.xin
        for pr in range(NQB // 2):
            for s in range(2):
                b = 2 * pr + s
                self.dma("sp", xin[:, s, :], self.xp[ti * T + b * 128: ti * T + (b + 1) * 128, :], [], [self.xin_r[s]])
            self._x_transposes(2 * pr, 2)

    def _x_transposes(self, b0, nb):
        xin, xres, ident = self.xin, self.xres, self.ident
        for c in range(KC):
            bank, br = self.psum()
            for j in range(nb):
                self.op("pe", (lambda bank, j, c: lambda e: e.transpose(bank[:, j * 128:(j + 1) * 128], xin[:, j, c * 128:(c + 1) * 128], ident[:]))(bank, j, c),
                        [self.xin_r[j], self.ident_r], [br])
            eng = "act" if c % 2 == 0 else "dve"
            self.cp(eng, xres[:, c, b0 * 128:(b0 + nb) * 128], bank[:, 0:nb * 128], [br], [self.xres_r[c]])

    def halo_stage(self):
        xin, xres, ident = self.xin, self.xres, self.ident
        self.dma("sp", xin[:, 0, :], self.xhalo_d, [], [self.xin_r[0]])
        for c in range(KC):
            bank, br = self.psum()
            self.op("pe", (lambda bank, c: lambda e: e.transpose(bank[:, 0:128], xin[:, 0, c * 128:(c + 1) * 128], ident[:]))(bank, c),
                    [self.xin_r[0], self.ident_r], [br])
            self.cp("act" if c % 2 == 0 else "dve", xres[:, c, 0:128], bank[:, 0:128], [br], [self.xres_r[c]])
        self.n = 128
        self.rmsnorm(V_AN, out_u0=16)
        self.n = T

    def load_xs(self):
        xin, xres, ident = self.xin, self.xres, self.ident
        self.dma("sp", xin[0:NS, 0, :], self.xs_d, [], [self.xin_r[0]])
        for c in range(KC):
            bank, br = self.psum()
            self.op("pe", (lambda bank, c: lambda e: e.transpose(bank[:, 0:NS], xin[0:NS, 0, c * 128:(c + 1) * 128], ident[0:NS, 0:NS]))(bank, c),
                    [self.xin_r[0], self.ident_r], [br])
            self.cp("act" if c % 2 == 0 else "dve", xres[:, c, 0:NS], bank[:, 0:NS], [br], [self.xres_r[c]])

    def rmsnorm(self, gcol, inplace=False, out_u0=None):
        n = self.n
        xres, sq, ones, vecs, cst = self.xres, self.sq, self.ones, self.vecs, self.cst
        bank, br = self.psum()
        for c in range(KC):
            s = self._sq_i % 3
            self._sq_i += 1
            self.tt("pool", sq[:, s, 0:n], xres[:, c, 0:n], xres[:, c, 0:n], ALU.mult, [self.xres_r[c]], [self.sq_r[s]])
            self.mm(bank[:, 0:n], ones[:], sq[:, s, 0:n], c == 0, c == KC - 1, [self.ones_r, self.sq_r[s]], [br])
        f1, f1r = self.ftmp()
        self.act(f1[:, 0:n], bank[:, 0:n], AF.Ln, [br, self.cst_r], [f1r], scale=1.0 / D, bias=cst[:, 0:1])
        self.act(f1[:, 0:n], f1[:, 0:n], AF.Exp, [f1r], [f1r], scale=-0.5)
        for c in range(KC):
            if out_u0 is not None:
                self.stt(self.U[:, out_u0 + c, 0:n], xres[:, c, 0:n], vecs[:, gcol + c:gcol + c + 1], f1[:, 0:n], ALU.mult, ALU.mult,
                         [self.xres_r[c], self.vecs_r, f1r], [self.U_r[out_u0 + c]])
            elif inplace:
                self.stt(xres[:, c, 0:n], xres[:, c, 0:n], vecs[:, gcol + c:gcol + c + 1], f1[:, 0:n], ALU.mult, ALU.mult,
                         [self.xres_r[c], self.vecs_r, f1r], [self.xres_r[c]])
            else:
                self.stt(self.h[:, c, 0:n], xres[:, c, 0:n], vecs[:, gcol + c:gcol + c + 1], f1[:, 0:n], ALU.mult, ALU.mult,
                         [self.xres_r[c], self.vecs_r, f1r], [self.h_r[c]])

    def proj(self, lhs_list, rhs_list, evac):
        n = self.n
        for o0 in range(0, len(lhs_list), 2):
            ocs = list(range(o0, min(o0 + 2, len(lhs_list))))
            banks = [self.psum() for _ in ocs]
            m = len(lhs_list[o0])
            for k in range(m):
                for oc, (bank, br) in zip(ocs, banks):
                    lk = lhs_list[oc]
                    self.mm(bank[:, 0:n], lk[k][0], rhs_list[k][0], k == 0, k == m - 1, [lk[k][1], rhs_list[k][1]], [br])
            for oc, (bank, br) in zip(ocs, banks):
                evac(oc, bank, br)

    def out_proj(self, wkey, k0):
        n = self.n
        U, xres = self.U, self.xres
        gk = [(U[:, k0 + k, 0:n], self.U_r[k0 + k]) for k in range(KC)]
        for p in range(2):
            w, wr = self.wnext(wkey, p)
            wv = w[:, :].rearrange("p (k n) -> p k n", k=KC)
            lhs = [[(wv[:, k, oc * 128:(oc + 1) * 128], wr) for k in range(KC)] for oc in range(4)]

            def ev_o(oc, bank, br, p=p):
                c = p * 4 + oc
                self.tt("dve", xres[:, c, 0:n], bank[:, 0:n], xres[:, c, 0:n], ALU.add, [br, self.xres_r[c]], [self.xres_r[c]])
            self.proj(lhs, gk, ev_o)

    def attention(self, ti, last):
        h, U, kT, va, btab, der = self.h, self.U, self.kT, self.va, self.btab, self.der
        hk = [(h[:, k, :], self.h_r[k]) for k in range(KC)]
        for p in range(2):
            w, wr = self.wnext("qkv", p)
            wv = w[:, :].rearrange("p (k n) -> p k n", k=KC)
            lhs = [[(wv[:, k, oc * 128:(oc + 1) * 128], wr) for k in range(KC)] for oc in range(4)]

            def ev_q(oc, bank, br, p=p):
                m = p * 4 + oc
                self.act(U[:, m, :], bank[:], AF.Copy, [br], [self.U_r[m]], scale=0.125)
            self.proj(lhs, hk, ev_q)
        w, wr = self.wnext("qkv", 2)
        wv = w[:, :].rearrange("p (k n) -> p k n", k=KC)
        lhs = [[(wv[:, k, cc * 128:(cc + 1) * 128], wr) for k in range(KC)] for cc in range(2)]

        def ev_k(cc, bank, br):
            self.cp("act", kT[:, cc, 128:128 + T], bank[:], [br], [self.kT_r[cc]])
        self.proj(lhs, hk, ev_k)
        if ti == 0:
            for cc in range(2):
                bank, br = self.psum()
                for k in range(KC):
                    self.mm(bank[:, 0:128], wv[:, k, cc * 128:(cc + 1) * 128], U[:, 16 + k, 0:128], k == 0, k == KC - 1, [wr, self.U_r[16 + k]], [br])
                self.cp("act", kT[:, cc, 0:128], bank[:, 0:128], [br], [self.kTh_r[cc]])
            bank, br = self.psum()
            for k in range(KC):
                self.mm(bank[:, 0:256], U[:, 16 + k, 0:128], wv[:, k, 256:512], k == 0, k == KC - 1, [self.U_r[16 + k], wr], [br])
            for j in range(NKV):
                vo = (j % 2) * 64
                self.cp("dve" if j % 2 == 0 else "act", va[:, 0, j, vo:vo + 64], bank[:, j * 64:(j + 1) * 64], [br], [self.va_r[0]])
        for b in range(NQB):
            bank, br = self.psum()
            for k in range(KC):
                self.mm(bank[:, 0:256], h[:, k, b * 128:(b + 1) * 128], wv[:, k, 256:512], k == 0, k == KC - 1, [self.h_r[k], wr], [br])
            for j in range(NKV):
                vo = (j % 2) * 64
                self.cp("dve" if j % 2 == 0 else "act", va[:, b + 1, j, vo:vo + 64], bank[:, j * 64:(j + 1) * 64], [br], [self.va_r[b + 1]])
            if last and b == NQB - 1:
                self.cp("act", self.kv32[:, 256:512], bank[:, 0:256], [br], [self.kv32b_r])
                bank2, br2 = self.psum()
                for k in range(KC):
                    self.mm(bank2[:, 0:256], h[:, k, b * 128:(b + 1) * 128], wv[:, k, 0:256], k == 0, k == KC - 1, [self.h_r[k], wr], [br2])
                self.cp("act", self.kv32[:, 0:256], bank2[:, 0:256], [br2], [self.kv32_r])
                self.out_evs.append(self.dma("sp", self.nk, self.kv32[:, 0:256], [self.kv32_r], []))
                self.out_evs.append(self.dma("sp", self.nv, self.kv32[:, 256:512], [self.kv32b_r], []))
        items = [(qb, j) for qb in range(NQB) for j in range(NKV)]
        pts = {}

        def scores(qb, j):
            Pp, half = j // 2, j % 2
            rows = slice(half * 64, half * 64 + 64)
            res = []
            for typ in range(2):
                bank, br = self.psum()
                kcol = qb * 128 + typ * 128
                kres = self.kTh_r[Pp] if (qb == 0 and typ == 0) else self.kT_r[Pp]
                self.mm(bank[:], kT[rows, Pp, kcol:kcol + 128], U[rows, 4 * Pp:4 * Pp + 4, qb * 128:(qb + 1) * 128], True, True,
                        [kres] + [self.U_r[4 * Pp + g] for g in range(4)], [br])
                f, fr = self.ftmp()
                self.tt("dve", f[:], bank[:], btab[:, typ * 4 + j, :], ALU.add, [br, self.btab_r], [fr])
                pi = 16 + (self._pt_i % 6)
                self._pt_i += 1
                if ti == 0 and qb == 0 and typ == 0:
                    self.act(U[:, pi, :], f[:], AF.Exp, [fr, self.hb_r], [self.U_r[pi]], bias=self.hb[:, 0:1])
                else:
                    self.act(U[:, pi, :], f[:], AF.Exp, [fr], [self.U_r[pi]])
                res.append(pi)
            pts[(qb, j)] = res

        def pv(qb, j):
            Pp, half = j // 2, j % 2
            orow = slice(half * 64, half * 64 + 64)
            drow = slice((1 - half) * 64, (1 - half) * 64 + 64)
            bank, br = self.psum()
            for typ in range(2):
                pi = pts[(qb, j)][typ]
                self.mm(bank[:], va[:, qb + typ, j, :], U[:, pi, :], typ == 0, typ == 1, [self.va_r[qb + typ], self.U_r[pi]], [br])
            f, fr = self.ftmp()
            fo = f[orow, :].rearrange("p (g q) -> p g q", g=4)
            esb = der[orow, 24 + 4 * j:24 + 4 * j + 4].unsqueeze(2).to_broadcast([64, 4, 128])
            self.tt("dve", fo, bank[drow, :].rearrange("p (g q) -> p g q", g=4), esb, ALU.add, [br, self.der_r], [fr])
            self.act(f[orow, :], f[orow, :], AF.Ln, [fr], [fr])
            self.act(f[orow, :], f[orow, :], AF.Exp, [fr], [fr], scale=-1.0)
            outv = U[orow, 8 + 4 * Pp:8 + 4 * Pp + 4, qb * 128:(qb + 1) * 128]
            self.tt("dve", outv, bank[orow, :].rearrange("p (g q) -> p g q", g=4), fo, ALU.mult,
                    [br, fr], [self.U_r[8 + 4 * Pp + g] for g in range(4)])

        scores(*items[0])
        for i in range(len(items)):
            if i + 1 < len(items):
                scores(*items[i + 1])
            pv(*items[i])
        for cc in range(2):
            self.cp("pool", kT[:, cc, 0:128], kT[:, cc, T:T + 128], [self.kT_r[cc]], [self.kTh_r[cc]])
        self.cp("pool", va[:, 0, :, :], va[:, NQB, :, :], [self.va_r[NQB]], [self.va_r[0]])
        self.out_proj("wo", 8)

    def attention_s(self):
        n = NS
        h, U, der = self.h, self.U, self.der
        q32, kn32, kv32, bts, ones32 = self.q32, self.kn32, self.kv32, self.bts, self.ones32
        Kc = self.btab[:].rearrange("p a b -> p (a b)").rearrange("p (s q k) -> p s q k", s=SB, q=2)
        Vc = self.xcT[:].rearrange("p c t -> p (c t)").rearrange("p (s j d) -> p s j d", s=SB, j=NKV)
        vn4 = self.trT[0:4, :, :].rearrange("p c t -> p (c t)").rearrange("p (s c) -> p s c", s=SB)
        self.dma("sp", self.btab[:].rearrange("p a b -> p (a b)"), self.ckT_d, [], [self.btab_r])
        self.dma("sp", Vc.rearrange("p s j d -> p s (j d)"), self.cv_d.rearrange("s p c -> p s c"), [], list(self.xc_r))
        hk = [(h[:, k, 0:n], self.h_r[k]) for k in range(KC)]
        for p in range(2):
            w, wr = self.wnext("qkv", p)
            wv = w[:, :].rearrange("p (k n) -> p k n", k=KC)
            lhs = [[(wv[:, k, oc * 128:(oc + 1) * 128], wr) for k in range(KC)] for oc in range(4)]

            def ev_q(oc, bank, br, p=p):
                m = p * 4 + oc
                self.act(q32[:, m, :], bank[:, 0:n], AF.Copy, [br], [self.q32_r], scale=0.125)
            self.proj(lhs, hk, ev_q)
        w, wr = self.wnext("qkv", 2)
        wv = w[:, :].rearrange("p (k n) -> p k n", k=KC)
        lhs = [[(wv[:, k, cc * 128:(cc + 1) * 128], wr) for k in range(KC)] for cc in range(2)]

        def ev_k(cc, bank, br):
            self.cp("act", kn32[:, cc, :], bank[:, 0:n], [br], [self.kn32_r])
        self.proj(lhs, hk, ev_k)
        for which, c0, rr in ((0, 0, self.kv32_r), (1, 256, self.kv32b_r)):
            bank, br = self.psum()
            for k in range(KC):
                self.mm(bank[0:n, 0:256], h[:, k, 0:n], wv[:, k, c0:c0 + 256], k == 0, k == KC - 1, [self.h_r[k], wr], [br])
            self.cp("act" if which == 0 else "dve", kv32[0:n, c0:c0 + 256], bank[0:n, 0:256], [br], [rr])
        kscr_r, vscr_r = Res("kscr"), Res("vscr")
        self.dma("sp", self.kscr, kv32[0:n, 0:256], [self.kv32_r], [kscr_r])
        self.dma("sp", self.vscr, kv32[0:n, 256:512], [self.kv32b_r], [vscr_r])
        self.out_evs.append(self.dma("sp", self.nks[:, 124:128, :], self.kscr.rearrange("(s t) c -> s t c", t=DEC_T), [kscr_r], [Res("nks_n")]))
        self.out_evs.append(self.dma("sp", self.nvs[:, 124:128, :], self.vscr.rearrange("(s t) c -> s t c", t=DEC_T), [vscr_r], [Res("nvs_n")]))
        self.dma("sp", vn4, self.vscr.rearrange("(s t) c -> t s c", t=DEC_T), [vscr_r], list(self.tr_r))
        SC = [self.psum(), self.psum()]
        SN = [self.psum(), self.psum()]
        for s in range(SB):
            hb_, sl = s // 8, s % 8
            for j in range(NKV):
                Pp, half = j // 2, j % 2
                rows = slice(half * 64, half * 64 + 64)
                col = sl * 64 + j * 16
                rhs = q32[rows, 4 * Pp:4 * Pp + 4, 4 * s:4 * s + 4]
                self.mm(SC[hb_][0][:, col:col + 16], Kc[rows, s, Pp, :], rhs, True, True, [self.btab_r, self.q32_r], [SC[hb_][1]])
                self.mm(SN[hb_][0][0:4, col:col + 16], kn32[rows, Pp, 4 * s:4 * s + 4], rhs, True, True, [self.kn32_r, self.q32_r], [SN[hb_][1]])
        PC, PN = [], []
        for hb_ in range(2):
            f, fr = self.ftmp()
            self.tt("dve", f[:, :].rearrange("p (s c) -> p s c", s=8), SC[hb_][0][:, :].rearrange("p (s c) -> p s c", s=8),
                    bts[:, 0:64].unsqueeze(1).to_broadcast([128, 8, 64]), ALU.add, [SC[hb_][1], self.bts_r], [fr])
            self.act(f[:, :], f[:, :], AF.Exp, [fr], [fr])
            PC.append((f, fr))
            g_, gr = self.ftmp()
            self.tt("dve", g_[0:4, :].rearrange("p (s c) -> p s c", s=8), SN[hb_][0][0:4, :].rearrange("p (s c) -> p s c", s=8),
                    bts[0:4, 64:128].unsqueeze(1).to_broadcast([4, 8, 64]), ALU.add, [SN[hb_][1], self.bts_r], [gr])
            self.act(g_[0:4, :], g_[0:4, :], AF.Exp, [gr], [gr])
            PN.append((g_, gr))
        OB = [self.psum(), self.psum()]
        DB = [self.psum(), self.psum()]
        for s in range(SB):
            hb_, sl = s // 8, s % 8
            pc, pcr = PC[hb_]
            pn, pnr = PN[hb_]
            for j in range(NKV):
                col = sl * 64 + j * 16
                self.mm(OB[hb_][0][0:64, col:col + 16], Vc[:, s, j, :], pc[:, col:col + 16], True, False, list(self.xc_r) + [pcr], [OB[hb_][1]])
                self.mm(OB[hb_][0][0:64, col:col + 16], vn4[0:4, s, j * 64:(j + 1) * 64], pn[0:4, col:col + 16], False, True,
                        list(self.tr_r) + [pnr], [OB[hb_][1]])
                self.mm(DB[hb_][0][0:64, col:col + 16], ones32[:, :], pc[:, col:col + 16], True, False, [self.ones32_r, pcr], [DB[hb_][1]])
                self.mm(DB[hb_][0][0:64, col:col + 16], ones32[0:4, :], pn[0:4, col:col + 16], False, True, [self.ones32_r, pnr], [DB[hb_][1]])
        for hb_ in range(2):
            for j in range(NKV):
                Pp, half = j // 2, j % 2
                orow = slice(half * 64, half * 64 + 64)
                f, fr = self.ftmp()
                fv = f[orow, 0:128].rearrange("p (s g t) -> p s g t", s=8, g=4)
                dv = DB[hb_][0][0:64, :].rearrange("p (s j c) -> p s j c", s=8, j=NKV)[:, :, j, :].rearrange("p s (g t) -> p s g t", g=4)
                ov = OB[hb_][0][0:64, :].rearrange("p (s j c) -> p s j c", s=8, j=NKV)[:, :, j, :].rearrange("p s (g t) -> p s g t", g=4)
                esb = der[orow, 24 + 4 * j:24 + 4 * j + 4].unsqueeze(1).unsqueeze(3).to_broadcast([64, 8, 4, DEC_T])
                self.tt("dve", fv, dv, esb, ALU.add, [DB[hb_][1], self.der_r], [fr])
                self.op("dve", (lambda f, orow: lambda e: e.reciprocal(out=f[orow, 0:128], in_=f[orow, 0:128]))(f, orow), [fr], [fr])
                outv = U[orow, 8 + 4 * Pp:8 + 4 * Pp + 4, hb_ * 32:(hb_ + 1) * 32].rearrange("p g (s t) -> p s g t", t=DEC_T)
                self.tt("dve", outv, ov, fv, ALU.mult, [OB[hb_][1], fr], [self.U_r[8 + 4 * Pp + g] for g in range(4)])
        self.out_proj("wo", 8)

    def ffn(self, layer):
        n = self.n
        h, U, xres = self.h, self.U, self.xres
        kin, kout = "fi%d" % layer, "fo%d" % layer
        for p in range(11):
            w, wr = self.wnext(kin, p)
            wv = w[:, :].rearrange("p (k n) -> p k n", k=KC)
            for cl in range(2):
                c = 2 * p + cl
                bg, bgr = self.psum()
                bu, bur = self.psum()
                for k in range(KC):
                    self.mm(bg[:, 0:n], wv[:, k, cl * 128:(cl + 1) * 128], h[:, k, 0:n], k == 0, k == KC - 1, [wr, self.h_r[k]], [bgr])
                    self.mm(bu[:, 0:n], wv[:, k, 256 + cl * 128:256 + (cl + 1) * 128], h[:, k, 0:n], k == 0, k == KC - 1, [wr, self.h_r[k]], [bur])
                f, fr = self.ftmp()
                self.act(f[:, 0:n], bg[:, 0:n], AF.Silu, [bgr], [fr])
                self.tt("dve", U[:, c, 0:n], bu[:, 0:n], f[:, 0:n], ALU.mult, [bur, fr], [self.U_r[c]])
        for oc in range(KC):
            w, wr = self.wnext(kout, oc)
            wv = w[:, 0:FC * 128].rearrange("p (k n) -> p k n", k=FC)
            bank, br = self.psum()
            for k in range(FC):
                self.mm(bank[:, 0:n], wv[:, k, :], U[:, k, 0:n], k == 0, k == FC - 1, [wr, self.U_r[k]], [br])
            self.tt("dve", xres[:, oc, 0:n], bank[:, 0:n], xres[:, oc, 0:n], ALU.add, [br, self.xres_r[oc]], [self.xres_r[oc]])

    def recurrent(self, smp, ti=None, mid_hook=None):
        n = self.n
        h, U, xres, xbt, vecs, der, cst = self.h, self.U, self.xres, self.xbt, self.vecs, self.der, self.cst
        hk = [(h[:, k, 0:n], self.h_r[k]) for k in range(KC)]

        def v3(ap):
            return ap.rearrange("p (s t) -> p s t", t=DEC_T)
        if smp:
            f, fr = self.ftmp()
            self.dma("sp", f[:, 0:KC * SB * 3], self.sconv_d, [], [fr])
            self.cp("pool", self.xps[:, :, :, 0:3], f[:, 0:KC * SB * 3].rearrange("p (c s r) -> p c s r", c=KC, s=SB), [fr], [self.xpsh_r])
        for p in range(4):
            w, wr = self.wnext("ri", p)
            wv = w[:, :].rearrange("p (k n) -> p k n", k=KC)
            lhs = [[(wv[:, k, oc * 128:(oc + 1) * 128], wr) for k in range(KC)] for oc in range(4)]

            def ev_i(oc, bank, br, p=p):
                c = (p % 2) * 4 + oc
                if p < 2:
                    self.act(U[:, c, 0:n], bank[:, 0:n], AF.Gelu_apprx_tanh, [br], [self.U_r[c]])
                elif smp:
                    self.cp("dve", self.xps[:, c, :, 3:7], v3(bank[:, 0:n]), [br], [self.xps_r[c]])
                else:
                    self.cp("dve", xbt[:, c, 3:3 + T], bank[:], [br], [self.xbt_r[c]])
            self.proj(lhs, hk, ev_i)
            if smp and p >= 2:
                bank, br = self.psum()
                for k in range(KC):
                    self.mm(bank[0:n, :], h[:, k, 0:n], wv[:, k, :], k == 0, k == KC - 1, [self.h_r[k], wr], [br])
                self.cp("act", self.yout[0:n, 0, (p - 2) * 512:(p - 1) * 512], bank[0:n, :], [br], [self.yout_r[0]])
        if (not smp) and ti == 0:
            sm = self.sm
            self.cp("dve", sm[:, 0:24].rearrange("p (c t) -> p c t", c=KC), xbt[:, :, 3:6], list(self.xbt_r), [self.sm_r])
            self.cp("dve", sm[:, 24:48].rearrange("p (c t) -> p c t", c=KC), U[:, 0:KC, 0:3], list(self.U_r[0:KC]) + [self.sm_r], [self.sm_r])
        if smp:
            xbs_r = Res("xbscr")
            self.dma("sp", self.xbscr, self.yout[0:n, 0, :], [self.yout_r[0]], [xbs_r])
            self.out_evs.append(self.dma("sp", self.ncs, self.xbscr.rearrange("(s t) c -> s t c", t=DEC_T)[:, 1:4, :], [xbs_r], [Res("ncs")]))
        for c in range(KC):
            xcc, xcr = self.xc[c], self.xc_r[c]
            if smp:
                rd = [self.xps_r[c], self.xpsh_r, self.vecs_r]
                xo = v3(xcc[:, 0:n])
                tap = lambda j, c=c: self.xps[:, c, :, j:j + DEC_T]
            else:
                rd = [self.xbt_r[c], self.xbh_r[c], self.vecs_r]
                xo = xcc[:, 0:n]
                tap = lambda j, c=c: xbt[:, c, j:j + T]
            self.act(xo, tap(3), AF.Identity, rd, [xcr], scale=vecs[:, V_CW + 24 + c:V_CW + 25 + c], bias=vecs[:, V_CB + c:V_CB + c + 1])
            for j in (2, 1, 0):
                self.stt(xo, tap(j), vecs[:, V_CW + 8 * j + c:V_CW + 8 * j + c + 1], xo, ALU.mult, ALU.add, rd + [xcr], [xcr])
            self.cp("act", U[:, 8 + c, 0:n], xcc[:, 0:n], [xcr], [self.U_r[8 + c]])
            if not smp:
                self.cp("pool", xbt[:, c, 0:3], xbt[:, c, T:T + 3], [self.xbt_r[c]], [self.xbh_r[c]])
        w, wr = self.wnext("gt", 0)
        wg = w[:, :].rearrange("p (g b k n) -> p g b k n", g=2, b=4, k=2)
        for oc in range(KC):
            blk, hf = oc // 2, oc % 2
            for gi in range(2):
                bank, br = self.psum()
                for k in range(2):
                    self.mm(bank[:, 0:n], wg[:, gi, blk, k, hf * 128:(hf + 1) * 128], U[:, 8 + 2 * blk + k, 0:n], k == 0, k == 1,
                            [wr, self.U_r[8 + 2 * blk + k]], [br])
                if gi == 0:
                    self.act(self.tr[oc][:, 0:n], bank[:, 0:n], AF.Tanh, [br, self.der_r], [self.tr_r[oc]], scale=0.5, bias=der[:, 8 + oc:9 + oc])
                else:
                    self.act(U[:, 16 + oc, 0:n], bank[:, 0:n], AF.Tanh, [br, self.der_r], [self.U_r[16 + oc]], scale=0.5, bias=der[:, 16 + oc:17 + oc])
        if mid_hook is not None:
            mid_hook()
        for oc in range(KC):
            tr, trr = self.tr[oc], self.tr_r[oc]
            self.act(tr[:, 0:n], tr[:, 0:n], AF.Exp, [trr, self.der_r], [trr], scale=der[:, oc:oc + 1], bias=der[:, oc:oc + 1])
        for oc in range(KC):
            tr, trr = self.tr[oc], self.tr_r[oc]
            xcc, xcr = self.xc[oc], self.xc_r[oc]
            s, sr = self.ftmp()
            self.tt("pool", s[:, 0:n], tr[:, 0:n], tr[:, 0:n], ALU.mult, [trr], [sr])
            self.act(s[:, 0:n], s[:, 0:n], AF.Ln, [sr, self.cst_r], [sr], scale=-1.0, bias=cst[:, 1:2])
            self.act(s[:, 0:n], s[:, 0:n], AF.Exp, [sr], [sr], scale=0.5)
            self.stt(xcc[:, 0:n], U[:, 16 + oc, 0:n], 1.0, xcc[:, 0:n], ALU.add, ALU.mult, [self.U_r[16 + oc], xcr], [xcr])
            self.stt(xcc[:, 0:n], xcc[:, 0:n], 0.5, s[:, 0:n], ALU.mult, ALU.mult, [xcr, sr], [xcr])
        for oc in range(KC):
            tr, trr = self.tr[oc], self.tr_r[oc]
            xcc, xcr = self.xc[oc], self.xc_r[oc]
            hs, hsr = self.ftmp()
            if smp:
                a3, b3 = v3(tr[:, 0:n]), v3(xcc[:, 0:n])
                t2, t2r = self.ftmp()
                self.tt("dve", t2[:, 0:SB], a3[:, :, 0], self.h0s[:, oc, :], ALU.mult, [trr, self.h0s_r], [t2r])
                self.tt("dve", b3[:, :, 0], b3[:, :, 0], t2[:, 0:SB], ALU.add, [xcr, t2r], [xcr])
                self.op("pool", (lambda a3: lambda e: e.memset(a3[:, :, 0:1], 0.0))(a3), [trr, t2r], [trr])
                self.op("dve", (lambda hs, tr, xcc: lambda e: e.tensor_tensor_scan(out=hs[:, 0:n], data0=tr[:, 0:n], data1=xcc[:, 0:n], initial=0.0,
                                                                                   op0=ALU.mult, op1=ALU.add))(hs, tr, xcc),
                        [trr, xcr], [hsr])
                self.cp("pool", self.hl[:, oc, :], v3(hs[:, 0:n])[:, :, DEC_T - 1], [hsr], [self.hl_r[oc]])
            else:
                if ti == 0:
                    self.op("pool", (lambda tr: lambda e: e.memset(tr[:, 0:3], 1.0))(tr), [trr], [trr])
                As, Asr = self.ftmp()
                zbc = cst[:, 2:3].to_broadcast([128, T])
                self.op("dve", (lambda As, tr, oc: lambda e: e.tensor_tensor_scan(out=As[:], data0=tr[:], data1=zbc, initial=self.acar[:, oc:oc + 1],
                                                                                 op0=ALU.mult, op1=ALU.add))(As, tr, oc),
                        [trr, self.cst_r, self.acar_r[oc]], [Asr])
                self.cp("dve", self.acar[:, oc:oc + 1], As[:, T - 1:T], [Asr], [self.acar_r[oc]])
                if ti == 0:
                    self.op("pool", (lambda tr: lambda e: e.memset(tr[:, 0:3], 0.0))(tr), [trr], [trr])
                    self.op("pool", (lambda xcc: lambda e: e.memset(xcc[:, 0:3], 0.0))(xcc), [xcr], [xcr])
                self.op("dve", (lambda hs, tr, xcc, oc: lambda e: e.tensor_tensor_scan(out=hs[:], data0=tr[:], data1=xcc[:], initial=self.hcar[:, oc:oc + 1],
                                                                                       op0=ALU.mult, op1=ALU.add))(hs, tr, xcc, oc),
                        [trr, xcr, self.hcar_r[oc]], [hsr])
                self.cp("dve", self.hcar[:, oc:oc + 1], hs[:, T - 1:T], [hsr], [self.hcar_r[oc]])
                self.tt("pool", U[:, 8 + oc, 0:n], As[:, 0:n], U[:, oc, 0:n], ALU.mult, [Asr, self.U_r[oc]], [self.U_r[8 + oc]])
            self.tt("dve", U[:, 16 + oc, 0:n], hs[:, 0:n], U[:, oc, 0:n], ALU.mult, [hsr, self.U_r[oc]], [self.U_r[16 + oc]])
        if smp:
            for half in range(2):
                bank, br = self.psum()
                for cc in range(4):
                    c = half * 4 + cc
                    self.op("pe", (lambda bank, cc, c: lambda e: e.transpose(bank[0:SB, cc * 128:(cc + 1) * 128], self.hl[:, c, :], self.ident[:]))(bank, cc, c),
                            [self.hl_r[c], self.ident_r], [br])
                self.cp("act", self.yout[0:SB, 1, half * 512:(half + 1) * 512], bank[0:SB, :], [br], [self.yout_r[1]])
            self.out_evs.append(self.dma("sp", self.nhs, self.yout[0:SB, 1, :], [self.yout_r[1]], []))
            self.out_proj("ro", 16)
        else:
            self.pqs_r[ti] = Res("pqs%d" % ti)
            self.dma("act", self.pqs[ti], U[:, 8:24, :].rearrange("p a t -> p (a t)"), list(self.U_r[8:24]), [self.pqs_r[ti]])

    def rec_down(self, ti):
        U, xres = self.U, self.xres
        self.dma("sp", xres[:, :, :].rearrange("p c t -> p (c t)"), self.x1s[ti], [self.x1s_r[ti]], list(self.xres_r))
        self.dma("sp", U[:, 8:24, :].rearrange("p a t -> p (a t)"), self.pqs[ti], [self.pqs_r[ti]], list(self.U_r[8:24]))
        if ti == 0:
            p3 = self.sm[:, 352:376].rearrange("p (c t) -> p c t", c=KC)
            self.cp("dve", U[:, 16:24, 0:3], p3, [self.sm_r], list(self.U_r[16:24]))
            self.op("pool", lambda e: e.memset(U[:, 8:16, 0:3], 0.0), [], list(self.U_r[8:16]))
        for oc in range(KC):
            self.stt(U[:, 16 + oc, :], U[:, 8 + oc, :], self.h2[:, oc:oc + 1], U[:, 16 + oc, :], ALU.mult, ALU.add,
                     [self.U_r[8 + oc], self.U_r[16 + oc], self.h2_r], [self.U_r[16 + oc]])
        self.out_proj("ro", 16)

    def fixup(self, h_in, halo3, rd):
        sm, smr, vecs, der, cst = self.sm, self.sm_r, self.vecs, self.der, self.cst

        def V(c0, k, c=KC):
            return sm[:, c0:c0 + c * k].rearrange("p (c t) -> p c t", c=c)
        xb3, gg3, xp6, xc3, tmp = V(0, 3), V(24, 3), V(48, 6), V(96, 3), V(120, 3)
        tr3, ti3, a3, s3, b3, hh3, p3 = V(144, 3), V(168, 3), V(192, 3), V(216, 3), V(240, 3), V(264, 3), V(352, 3)
        R = [smr]
        self.cp("dve", xp6[:, :, 0:3], halo3, R + rd, R)
        self.cp("dve", xp6[:, :, 3:6], xb3, R, R)

        def wj(j):
            return vecs[:, V_CW + 8 * j:V_CW + 8 * j + 8].unsqueeze(2).to_broadcast([128, KC, 3])
        self.tt("dve", xc3, xp6[:, :, 3:6], wj(3), ALU.mult, R + [self.vecs_r], R)
        for j in (2, 1, 0):
            self.tt("dve", tmp, xp6[:, :, j:j + 3], wj(j), ALU.mult, R + [self.vecs_r], R)
            self.tt("dve", xc3, xc3, tmp, ALU.add, R, R)
        self.tt("dve", xc3, xc3, vecs[:, V_CB:V_CB + 8].unsqueeze(2).to_broadcast([128, KC, 3]), ALU.add, R + [self.vecs_r], R)
        xcb = self.sq[:, 0, 0:24].rearrange("p (c t) -> p c t", c=KC)
        self.cp("dve", xcb, xc3, R, [self.sq_r[0]])
        w, wr = self.wnext("gt", 0)
        wg = w[:, :].rearrange("p (g b k n) -> p g b k n", g=2, b=4, k=2)
        bank, br = self.psum()
        for gi in range(2):
            for oc in range(KC):
                blk, hf = oc // 2, oc % 2
                c0 = (gi * 8 + oc) * 3
                for k in range(2):
                    self.mm(bank[:, c0:c0 + 3], wg[:, gi, blk, k, hf * 128:(hf + 1) * 128], xcb[:, 2 * blk + k, :], k == 0, k == 1, [wr, self.sq_r[0]], [br])
        zt = V(144, 3, 16)
        self.tt("dve", zt, bank[:, 0:48].rearrange("p (c t) -> p c t", c=16), vecs[:, V_BRG:V_BRG + 16].unsqueeze(2).to_broadcast([128, 16, 3]),
                ALU.add, [br, self.vecs_r] + R, R)
        self.act(sm[:, 144:192], sm[:, 144:192], AF.Tanh, R, R, scale=0.5)
        cvh = der[:, 0:8].unsqueeze(2).to_broadcast([128, KC, 3])
        self.tt("dve", a3, tr3, cvh, ALU.mult, R + [self.der_r], R)
        self.tt("dve", a3, a3, cvh, ALU.add, R + [self.der_r], R)
        self.act(sm[:, 192:216], sm[:, 192:216], AF.Exp, R, R)
        self.tt("dve", s3, a3, a3, ALU.mult, R, R)
        self.act(sm[:, 216:240], sm[:, 216:240], AF.Ln, R + [self.cst_r], R, scale=-1.0, bias=cst[:, 1:2])
        self.act(sm[:, 216:240], sm[:, 216:240], AF.Exp, R, R, scale=0.5)
        self.stt(b3, ti3, 1.0, xc3, ALU.add, ALU.mult, R, R)
        self.stt(b3, b3, 0.5, s3, ALU.mult, ALU.mult, R, R)
        self.tt("dve", hh3[:, :, 0], a3[:, :, 0], h_in, ALU.mult, R + rd, R)
        self.tt("dve", hh3[:, :, 0], hh3[:, :, 0], b3[:, :, 0], ALU.add, R, R)
        for t in (1, 2):
            self.tt("dve", hh3[:, :, t], a3[:, :, t], hh3[:, :, t - 1], ALU.mult, R, R)
            self.tt("dve", hh3[:, :, t], hh3[:, :, t], b3[:, :, t], ALU.add, R, R)
        self.cp("dve", self.h2[:, :], hh3[:, :, 2], R, [self.h2_r])
        self.tt("dve", p3, hh3, gg3, ALU.mult, R, R)

    def publish(self):
        sm, smr = self.sm, self.sm_r
        pub = sm[:, 288:288 + XW]
        allc = list(self.acar_r) + list(self.hcar_r)
        self.tt("dve", pub[:, 0:8], self.acar[:, :], self.h2[:, :], ALU.mult, allc + [self.h2_r, smr], [smr])
        self.tt("dve", pub[:, 0:8], pub[:, 0:8], self.hcar[:, :], ALU.add, allc + [smr], [smr])
        self.cp("dve", pub[:, 8:32].rearrange("p (c t) -> p c t", c=KC), self.xbt[:, :, 0:3], list(self.xbh_r) + [smr], [smr])
        self.bounce_r, self.gath_r = Res("bounce"), Res("gath")
        self.dma("sp", self.bounce, pub, [smr], [self.bounce_r])
        bounce, gath = self.bounce, self.gath
        self.P.op("pool", lambda e: e.collective_compute("AllGather", op=ALU.bypass, replica_groups=[list(range(8))],
                                                         ins=[bounce.opt()], outs=[gath.opt()]),
                  [self.bounce_r], [self.gath_r], dma=True, inc=1)

    def collect(self):
        sm, smr, G, sel = self.sm, self.sm_r, self.G, self.sel
        self.dma("sp", G[:], self.gath.rearrange("(r p) w -> p r w", p=128), [self.gath_r], [self.G_r])
        pred = sm[:, 320:320 + XW]
        self.op("dve", lambda e: e.tensor_scalar(out=pred, in0=G[:, 0, :], scalar1=sel[:, 0:1], scalar2=None, op0=ALU.mult),
                [self.G_r, self.sel_r, smr], [smr])
        for r in range(1, 8):
            self.stt(pred, G[:, r, :], sel[:, r:r + 1], pred, ALU.mult, ALU.add, [self.G_r, self.sel_r, smr], [smr])
        return pred[:, 0:8], pred[:, 8:32].rearrange("p (c t) -> p c t", c=KC)

    def store_y(self, ti):
        xres, yout, ident = self.xres, self.yout, self.ident
        for b in range(NQB):
            s = b % 2
            for half in range(2):
                bank, br = self.psum()
                for cc in range(4):
                    c = half * 4 + cc
                    self.op("pe", (lambda bank, cc, c, b: lambda e: e.transpose(bank[:, cc * 128:(cc + 1) * 128], xres[:, c, b * 128:(b + 1) * 128], ident[:]))(bank, cc, c, b),
                            [self.xres_r[c], self.ident_r], [br])
                eng = "act" if half == 0 else "dve"
                self.cp(eng, yout[:, s, half * 512:(half + 1) * 512], bank[:], [br], [self.yout_r[s]])
            self.out_evs.append(self.dma("sp", self.yp[ti * T + b * 128: ti * T + (b + 1) * 128, :], yout[:, s, :], [self.yout_r[s]], []))

    def store_ys(self):
        xres, yout, ident = self.xres, self.yout, self.ident
        for half in range(2):
            bank, br = self.psum()
            for cc in range(4):
                c = half * 4 + cc
                self.op("pe", (lambda bank, cc, c: lambda e: e.transpose(bank[0:NS, cc * 128:(cc + 1) * 128], xres[:, c, 0:NS], ident[:]))(bank, cc, c),
                        [self.xres_r[c], self.ident_r], [br])
            self.cp("act" if half == 0 else "dve", yout[0:NS, 0, half * 512:(half + 1) * 512], bank[0:NS, :], [br], [self.yout_r[0]])
        self.out_evs.append(self.dma("sp", self.ys, yout[0:NS, 0, :], [self.yout_r[0]], []))

    def store_state(self):
        for c in range(KC):
            self.out_evs.append(self.P.op("sp", (lambda c: lambda e: e.dma_start(
                out=self.ncv[:, c * 128:(c + 1) * 128].rearrange("r p -> p r"), in_=self.xbt[:, c, 0:3], allow_slow_non_contiguous=True))(c),
                [self.xbh_r[c]], [], dma=True))
        self.out_evs.append(self.P.op("sp", lambda e: e.dma_start(out=self.nh.rearrange("(c p) -> p c", p=128), in_=self.hcar[:, :], allow_slow_non_contiguous=True),
                                      self.hcar_r, [], dma=True))

    def build(self):
        self.declare()
        self.x1s_r, self.pqs_r = {}, {}
        self.setup_consts()
        self.setup_weights()
        nt = self.n_tiles
        cst = self.cst
        self.halo_stage()
        self.load_x(0)
        self.rmsnorm(V_AN)
        for ti in range(nt):
            last = ti == nt - 1
            self.n = T
            self.attention(ti, last)
            self.rmsnorm(V_FN0)
            self.ffn(0)
            self.x1s_r[ti] = Res("x1s%d" % ti)
            self.dma("act", self.x1s[ti], self.xres[:, :, :].rearrange("p c t -> p (c t)"), list(self.xres_r), [self.x1s_r[ti]])
            self.rmsnorm(V_RN)
            hook = (lambda ti=ti: (self.load_x(ti + 1), self.rmsnorm(V_AN))) if ti + 1 < nt else None
            self.recurrent(False, ti, mid_hook=hook)
        self.fixup(cst[:, 2:3].to_broadcast([128, KC]), cst[:, 2:3].unsqueeze(2).to_broadcast([128, KC, 3]), [self.cst_r])
        self.publish()
        if self.sample:
            self.n = NS
            self.load_xs()
            self.rmsnorm(V_AN)
            self.attention_s()
            self.rmsnorm(V_FN0)
            self.ffn(0)
            self.rmsnorm(V_RN)
            self.recurrent(True)
            self.rmsnorm(V_FN1)
            self.ffn(1)
            self.rmsnorm(V_FIN, inplace=True)
            self.store_ys()
        h_in, halo3 = self.collect()
        self.fixup(h_in, halo3, [])
        self.n = T
        for ti in range(nt):
            self.rec_down(ti)
            self.rmsnorm(V_FN1)
            self.ffn(1)
            self.rmsnorm(V_FIN, inplace=True)
            self.store_y(ti)
        allc = list(self.acar_r) + list(self.hcar_r)
        f, fr = self.ftmp()
        self.tt("dve", f[:, 0:8], self.acar[:, :], self.h2[:, :], ALU.mult, allc + [self.h2_r], [fr])
        self.tt("dve", self.hcar[:, :], self.hcar[:, :], f[:, 0:8], ALU.add, allc + [fr], list(self.hcar_r))
        self.store_state()
        self.P.final_wait("sp", self.out_evs)
        self.P.emit()
        self.st.close()
        return self.nc


def _kmajor(w, ncols_piece):
    K, N = w.shape
    kc = K // 128
    npc = N // ncols_piece
    a = w.reshape(kc, 128, npc, ncols_piece).transpose(1, 2, 0, 3)
    return np.ascontiguousarray(a).reshape(128, npc * kc * ncols_piece)


def _head_perm():
    cols = []
    for m in range(8):
        Pp, g = m // 4, m % 4
        a = 8 * Pp + g
        b = 8 * Pp + 4 + g
        cols += list(range(a * 64, a * 64 + 64)) + list(range(b * 64, b * 64 + 64))
    return np.array(cols)


def prep_weights(inp):
    f = np.float32
    perm = _head_perm()
    wqkv = np.asarray(inp["w_qkv"][0], f)
    wq = wqkv[:, :1024][:, perm]
    wkv = wqkv[:, 1024:1536]
    out = {}
    out["w_qkv"] = _kmajor(np.concatenate([wq, wkv], axis=1), 512)
    out["w_wo"] = _kmajor(np.asarray(inp["w_attn_out"][0], f)[perm, :], 512)
    for l in range(2):
        wi = np.asarray(inp["w_ffn_in"][l], f)
        g, u = wi[:, :DFF], wi[:, DFF:]
        cols = []
        for p in range(11):
            cols.append(g[:, p * 256:(p + 1) * 256])
            cols.append(u[:, p * 256:(p + 1) * 256])
        out["w_fi%d" % l] = _kmajor(np.concatenate(cols, axis=1), 512)
        out["w_fo%d" % l] = _kmajor(np.asarray(inp["w_ffn_out"][l], f), 128)
    out["w_ri"] = _kmajor(np.asarray(inp["w_rec_in"][0], f), 512)
    wg = np.stack([np.asarray(inp["w_rgate"][0], f), np.asarray(inp["w_igate"][0], f)])
    wg = wg.reshape(2, 4, 2, 128, 256).transpose(3, 0, 1, 2, 4)
    out["w_gt"] = np.ascontiguousarray(wg).reshape(128, 4096)
    out["w_ro"] = _kmajor(np.asarray(inp["w_rec_out"][0], f), 512)
    return out


def prep_vecs(inp):
    f = np.float32
    v = np.zeros((128, NV), f)

    def colmaj(x):
        return np.asarray(x, f).reshape(8, 128).T
    v[:, V_AN:V_AN + 8] = colmaj(inp["attn_norm"][0])
    v[:, V_FN0:V_FN0 + 8] = colmaj(inp["ffn_norm"][0])
    v[:, V_RN:V_RN + 8] = colmaj(inp["rec_norm"][0])
    v[:, V_FN1:V_FN1 + 8] = colmaj(inp["ffn_norm"][1])
    v[:, V_FIN:V_FIN + 8] = colmaj(inp["final_norm"])
    for j in range(4):
        v[:, V_CW + 8 * j:V_CW + 8 * j + 8] = colmaj(inp["conv_w"][0][j])
    v[:, V_CB:V_CB + 8] = colmaj(inp["conv_b"][0])
    v[:, V_BRG:V_BRG + 8] = colmaj(inp["b_rgate"][0])
    v[:, V_BIG:V_BIG + 8] = colmaj(inp["b_igate"][0])
    v[:, V_LAM:V_LAM + 8] = colmaj(inp["lru_lambda"][0])
    v[:, V_SINK:V_SINK + 16] = np.asarray(inp["attn_sinks"][0], f)[None, :]
    return v


def bias_table():
    slopes = (2.0 ** (-8.0 * np.arange(1, NH + 1, dtype=np.float64) / NH))
    key = np.arange(128)[:, None]
    q = np.arange(128)[None, :]
    tab = np.zeros((128, 2, NKV, 4, 128), np.float32)
    for typ in range(2):
        dist = q - key + (128 if typ == 0 else 0)
        valid = (dist >= 0) & (dist <= 128)
        for j in range(NKV):
            for g in range(4):
                hh = 4 * j + g
                tab[:, typ, j, g, :] = np.where(valid, -slopes[hh] * dist, NEG).astype(np.float32)
    return tab.reshape(128, 8 * 512)


_CACHE = {}


def bias_table_s():
    slopes = (2.0 ** (-8.0 * np.arange(1, NH + 1, dtype=np.float64) / NH))
    tab = np.zeros((128, 128), np.float32)
    pos = np.arange(128)[:, None]
    for j in range(NKV):
        for g in range(4):
            for t in range(DEC_T):
                col = j * 16 + g * 4 + t
                sl = slopes[4 * j + g]
                tab[:, col] = np.where(pos[:, 0] >= t, -sl * (t + 128 - pos[:, 0]), NEG)
                for nn in range(DEC_T):
                    tab[nn, 64 + col] = (-sl * (t - nn)) if nn <= t else NEG
    return tab


def run_all(inp, n_tiles, sample=True):
    key = ("p", n_tiles, sample)
    if key not in _CACHE:
        _CACHE[key] = Builder(n_tiles, sample).build()
    nc = _CACHE[key]
    f = np.float32
    wts = prep_weights(inp)
    vecs = prep_vecs(inp)
    bt = bias_table()
    bts = bias_table_s()
    L = n_tiles * T
    in_maps = []
    for c in range(8):
        b, hh = c // 2, c % 2
        m = dict(wts)
        m["vecs"] = vecs
        m["btab"] = bt
        m["hb"] = np.full((128, 1), NEG if hh == 0 else 0.0, f)
        sel = np.zeros((128, 8), f)
        if hh == 1:
            sel[:, c - 1] = 1.0
        m["sel"] = sel
        xp = np.asarray(inp["x_prompt"][b], f)
        m["xp"] = np.ascontiguousarray(xp[hh * L:(hh + 1) * L])
        m["xhalo"] = np.ascontiguousarray(xp[L - 128:L]) if hh == 1 else np.zeros((128, D), f)
        if sample:
            b0 = c * SB
            m["xs"] = np.ascontiguousarray(np.asarray(inp["x_sample"][b0:b0 + SB], f).reshape(NS, D))
            ck = np.asarray(inp["cache_k"][0, b0:b0 + SB], f)
            cv = np.asarray(inp["cache_v"][0, b0:b0 + SB], f)
            m["ck"] = np.ascontiguousarray(ck.reshape(SB, 128, 256))
            m["cv"] = np.ascontiguousarray(cv.reshape(SB, 128, 256))
            ckT = ck.reshape(SB, 128, 2, 2, HD).transpose(3, 4, 0, 2, 1)
            m["ckT"] = np.ascontiguousarray(ckT).reshape(128, SB * 2 * 128)
            sc = np.asarray(inp["state_conv"][0, b0:b0 + SB], f).reshape(SB, 3, KC, 128).transpose(3, 2, 0, 1)
            m["sconv"] = np.ascontiguousarray(sc).reshape(128, KC * SB * 3)
            sh = np.asarray(inp["state_h"][0, b0:b0 + SB], f).reshape(SB, KC, 128).transpose(2, 1, 0)
            m["sh0"] = np.ascontiguousarray(sh).reshape(128, KC * SB)
            m["bts"] = bts
        in_maps.append(m)
    res = run_bass_kernel_spmd(nc, in_maps, core_ids=list(range(8)))
    return res.results


def kernel(**inp):
    n_tiles = SEQ // T // 2
    r = run_all(inp, n_tiles, sample=True)
    f = np.float32
    y_prompt = np.stack([np.concatenate([r[2 * b]["yp"], r[2 * b + 1]["yp"]]) for b in range(4)]).astype(f)
    nk = np.stack([r[2 * b + 1]["nk"].reshape(128, 4, 64) for b in range(4)])[None].astype(f)
    nv = np.stack([r[2 * b + 1]["nv"].reshape(128, 4, 64) for b in range(4)])[None].astype(f)
    ncv = np.stack([r[2 * b + 1]["ncv"] for b in range(4)])[None].astype(f)
    nh = np.stack([r[2 * b + 1]["nh"] for b in range(4)])[None].astype(f)
    y_sample = np.concatenate([r[c]["ys"].reshape(SB, DEC_T, D) for c in range(8)]).astype(f)
    nks = np.concatenate([r[c]["nks"].reshape(SB, 128, 4, 64) for c in range(8)])[None].astype(f)
    nvs = np.concatenate([r[c]["nvs"].reshape(SB, 128, 4, 64) for c in range(8)])[None].astype(f)
    ncs = np.concatenate([r[c]["ncs"] for c in range(8)])[None].astype(f)
    nhs = np.concatenate([r[c]["nhs"] for c in range(8)])[None].astype(f)
    return (y_prompt, y_sample, nk, nv, nks, nvs, ncv, nh, ncs, nhs)
```
